# Optimizing a Trainium2 kernel written in Bass

```python
import jax, jax.numpy as jnp
from jax import lax
import numpy as np

D_MODEL = 1024
BATCH = 16
SEQ = 4096
DEPTH = 2
DEC_BATCH = 32
DEC_SEQ = 64
PAST_LEN = 2048

CHUNK = 64
D_MIX = D_MODEL
ATTN_WIDTH = D_MIX // 2
CONV_CH = D_MIX - ATTN_WIDTH
HEAD_DIM = 64
N_HEADS = ATTN_WIDTH // HEAD_DIM
KV_HEADS = 2
GROUP = N_HEADS // KV_HEADS
WINDOW = 128
W_CHUNKS = WINDOW // CHUNK
CONV_WIDTH = 31
D_FF = 4 * D_MODEL
EPS = 1e-5
Q_DIM = N_HEADS * HEAD_DIM
KV_DIM = KV_HEADS * HEAD_DIM
IN_DIM = Q_DIM + 2 * KV_DIM + 2 * CONV_CH
SPLITS = (Q_DIM, Q_DIM + KV_DIM, Q_DIM + 2 * KV_DIM, Q_DIM + 2 * KV_DIM + CONV_CH)
NEG_INF = -1e30

kernel_name = "hymba_swa_sink_conformer_conv_stream_step"


def _rmsnorm(x, g):
    xf = x.astype(jnp.float32)
    y = xf * lax.rsqrt(jnp.mean(xf * xf, axis=-1, keepdims=True) + EPS)
    return (y * g.astype(jnp.float32)).astype(x.dtype)


def _layernorm(x, g, b):
    xf = x.astype(jnp.float32)
    mu = jnp.mean(xf, axis=-1, keepdims=True)
    xc = xf - mu
    var = jnp.mean(xc * xc, axis=-1, keepdims=True)
    y = xc * lax.rsqrt(var + EPS) * g.astype(jnp.float32) + b.astype(jnp.float32)
    return y.astype(x.dtype)


def _sink_softmax(s, sink):
    sk = sink.astype(jnp.float32)[:, :, None, None]
    m = jnp.maximum(jnp.max(s, axis=-1, keepdims=True), sk)
    p = jnp.exp(s - m)
    return p / (jnp.sum(p, axis=-1, keepdims=True) + jnp.exp(sk - m))


def _swa_prompt(q, k, v, sink):
    B, S = q.shape[:2]
    nc = S // CHUNK
    band = (W_CHUNKS + 1) * CHUNK
    qb = (q * (HEAD_DIM ** -0.5)).reshape(B, nc, CHUNK, KV_HEADS, GROUP, HEAD_DIM)
    pad = ((0, 0), (W_CHUNKS * CHUNK, 0), (0, 0), (0, 0))
    kc = jnp.pad(k, pad).reshape(B, nc + W_CHUNKS, CHUNK, KV_HEADS, HEAD_DIM)
    vc = jnp.pad(v, pad).reshape(B, nc + W_CHUNKS, CHUNK, KV_HEADS, HEAD_DIM)
    kb = jnp.concatenate([kc[:, j:j + nc] for j in range(W_CHUNKS + 1)], axis=2)
    vb = jnp.concatenate([vc[:, j:j + nc] for j in range(W_CHUNKS + 1)], axis=2)
    s = jnp.einsum('bcqkgd,bcjkd->bckgqj', qb, kb).astype(jnp.float32)
    key_chunk = jnp.arange(nc)[:, None] - W_CHUNKS + jnp.arange(band)[None, :] // CHUNK
    valid = key_chunk >= 0
    s = jnp.where(valid[None, :, None, None, None, :], s, NEG_INF)
    p = _sink_softmax(s, sink).astype(v.dtype)
    o = jnp.einsum('bckgqj,bcjkd->bcqkgd', p, vb).reshape(B, S, Q_DIM)
    keep = min(WINDOW, S)
    return o, k[:, S - keep:], v[:, S - keep:]


def _swa_sample(q, k, v, sink, cache_k, cache_v):
    B, T = q.shape[:2]
    kk = jnp.concatenate([cache_k, k], axis=1)
    vv = jnp.concatenate([cache_v, v], axis=1)
    qg = (q * (HEAD_DIM ** -0.5)).reshape(B, T, KV_HEADS, GROUP, HEAD_DIM)
    s = jnp.einsum('btkgd,bjkd->bkgtj', qg, kk).astype(jnp.float32)
    p = _sink_softmax(s, sink).astype(v.dtype)
    o = jnp.einsum('bkgtj,bjkd->btkgd', p, vv).reshape(B, T, Q_DIM)
    keep = cache_k.shape[1]
    return o, kk[:, -keep:], vv[:, -keep:]


def _conv_module(a, g, hist, conv_w, conv_b, ln_g, ln_b):
    u = a * jax.nn.sigmoid(g)
    full = jnp.concatenate([hist, u], axis=1)
    h = lax.conv_general_dilated(full, conv_w[:, None, :], window_strides=(1,), padding='VALID',
                                 dimension_numbers=('NWC', 'WIO', 'NWC'),
                                 feature_group_count=CONV_CH) + conv_b
    h = jax.nn.silu(_layernorm(h, ln_g, ln_b))
    return h, full[:, -(CONV_WIDTH - 1):]


def _layer(x, cache_k, cache_v, conv_hist, norm1, w_in, sink, conv_w, conv_b, ln_g, ln_b,
           w_out, norm2, w_up, w_down):
    B, T, _ = x.shape
    xn = _rmsnorm(x, norm1)
    proj = xn @ w_in
    q, k, v, a, g = jnp.split(proj, SPLITS, axis=-1)
    q = q.reshape(B, T, N_HEADS, HEAD_DIM)
    k = k.reshape(B, T, KV_HEADS, HEAD_DIM)
    v = v.reshape(B, T, KV_HEADS, HEAD_DIM)
    sink_g = sink.reshape(KV_HEADS, GROUP)
    if cache_k is None:
        o_attn, k_new, v_new = _swa_prompt(q, k, v, sink_g)
    else:
        o_attn, k_new, v_new = _swa_sample(q, k, v, sink_g, cache_k, cache_v)
    o_conv, conv_new = _conv_module(a, g, conv_hist, conv_w, conv_b, ln_g, ln_b)
    x = x + jnp.concatenate([o_attn, o_conv], axis=-1) @ w_out
    hn = _rmsnorm(x, norm2)
    x = x + jnp.square(jax.nn.relu(hn @ w_up)) @ w_down
    return x, k_new, v_new, conv_new


def setup_inputs(seed: int = 0) -> dict:
    key = jax.random.key(seed)
    ks = jax.random.split(key, 20)
    f32 = jnp.float32
    win = min(WINDOW, PAST_LEN)
    n = jax.random.normal
    return {
        "x_prompt": n(ks[0], (BATCH, SEQ, D_MODEL), f32),
        "x_sample": n(ks[1], (DEC_BATCH, DEC_SEQ, D_MODEL), f32),
        "cache_k": n(ks[2], (DEPTH, DEC_BATCH, win, KV_HEADS, HEAD_DIM), f32),
        "cache_v": n(ks[3], (DEPTH, DEC_BATCH, win, KV_HEADS, HEAD_DIM), f32),
        "state_conv": 0.5 * n(ks[4], (DEPTH, DEC_BATCH, CONV_WIDTH - 1, CONV_CH), f32),
        "norm1": 1.0 + 0.05 * n(ks[5], (DEPTH, D_MODEL), f32),
        "w_in": n(ks[6], (DEPTH, D_MODEL, IN_DIM), f32) * D_MODEL ** -0.5,
        "attn_sink": 0.5 * n(ks[7], (DEPTH, N_HEADS), f32),
        "conv_w": n(ks[8], (DEPTH, CONV_WIDTH, CONV_CH), f32) * CONV_WIDTH ** -0.5,
        "conv_b": 0.01 * n(ks[9], (DEPTH, CONV_CH), f32),
        "conv_ln_g": 1.0 + 0.05 * n(ks[10], (DEPTH, CONV_CH), f32),
        "conv_ln_b": 0.01 * n(ks[11], (DEPTH, CONV_CH), f32),
        "w_out": n(ks[12], (DEPTH, D_MIX, D_MODEL), f32) * D_MIX ** -0.5,
        "norm2": 1.0 + 0.05 * n(ks[13], (DEPTH, D_MODEL), f32),
        "w_up": n(ks[14], (DEPTH, D_MODEL, D_FF), f32) * D_MODEL ** -0.5,
        "w_down": n(ks[15], (DEPTH, D_FF, D_MODEL), f32) * D_FF ** -0.5,
        "final_norm": 1.0 + 0.05 * n(ks[16], (D_MODEL,), f32),
    }


def reference(x_prompt, x_sample, cache_k, cache_v, state_conv, norm1, w_in, attn_sink, conv_w,
              conv_b, conv_ln_g, conv_ln_b, w_out, norm2, w_up, w_down, final_norm):
    yp, ys = x_prompt, x_sample
    kp_l, vp_l, cp_l, ks_l, vs_l, cs_l = [], [], [], [], [], []
    for l in range(DEPTH):
        params = (norm1[l], w_in[l], attn_sink[l], conv_w[l], conv_b[l], conv_ln_g[l],
                  conv_ln_b[l], w_out[l], norm2[l], w_up[l], w_down[l])
        hist0 = jnp.zeros((yp.shape[0], CONV_WIDTH - 1, CONV_CH), yp.dtype)
        yp, kp, vp, cp = _layer(yp, None, None, hist0, *params)
        ys, k_s, v_s, c_s = _layer(ys, cache_k[l], cache_v[l], state_conv[l], *params)
        kp_l.append(kp); vp_l.append(vp); cp_l.append(cp)
        ks_l.append(k_s); vs_l.append(v_s); cs_l.append(c_s)
    y_prompt = _rmsnorm(yp, final_norm)
    y_sample = _rmsnorm(ys, final_norm)
    new_k_prompt = jnp.stack(kp_l)
    new_v_prompt = jnp.stack(vp_l)
    new_conv_prompt = jnp.stack(cp_l)
    new_k_sample = jnp.stack(ks_l)
    new_v_sample = jnp.stack(vs_l)
    new_conv_sample = jnp.stack(cs_l)
    return (y_prompt, y_sample, new_k_prompt, new_v_prompt, new_conv_prompt,
            new_k_sample, new_v_sample, new_conv_sample)
```

```python
import numpy as np
from contextlib import ExitStack
import concourse.bass as bass
import concourse.mybir as mybir
from concourse.bass_utils import run_bass_kernel_spmd

F32 = mybir.dt.float32
BF16 = mybir.dt.bfloat16
AF = mybir.ActivationFunctionType
ALU = mybir.AluOpType

ENGS = ("pe", "act", "dve", "pool", "sp")
NCORES = 8
L = 2
D = 1024
NBLK = 22
EPS = 1e-5
TP, TD = 25, 6
STRICT_SAME_ENGINE = True


class Res:
    __slots__ = ("name", "lw", "rd")

    def __init__(self, name):
        self.name = name
        self.lw = None
        self.rd = []


class Op:
    __slots__ = ("eng", "emit", "deps", "idx", "needed", "dma", "sem", "semval", "rank", "prev_same_sem")

    def __init__(self, eng, emit, dma):
        self.eng = eng
        self.emit = emit
        self.deps = []
        self.needed = False
        self.dma = dma
        self.sem = None
        self.semval = 0
        self.rank = 0
        self.prev_same_sem = None
        self.idx = 0


class Tracker:
    def __init__(self, n_dma_sems=12):
        self.ops = {e: [] for e in ENGS}
        self.n_dma_sems = n_dma_sems
        self.dma_count = {e: 0 for e in ENGS}
        self.dma_last = {}

    def op(self, eng, emit, reads=(), writes=(), dma=False):
        o = Op(eng, emit, dma)
        deps = {}
        for r in reads:
            if r.lw is not None:
                deps[id(r.lw)] = (r.lw, "raw")
        for w in writes:
            if w.lw is not None and id(w.lw) not in deps:
                deps[id(w.lw)] = (w.lw, "waw")
            for rr in w.rd:
                if id(rr) not in deps:
                    deps[id(rr)] = (rr, "war")
        best = {}
        for d, kind in deps.values():
            if (not d.dma) and d.eng == eng and not dma:
                if eng == "pe" or (kind != "raw" and not STRICT_SAME_ENGINE):
                    continue
            if d.dma:
                o.deps.append(d)
            else:
                b = best.get(d.eng)
                if b is None or d.idx > b.idx:
                    best[d.eng] = d
        o.deps.extend(best.values())
        if dma:
            n = self.dma_count[eng]
            self.dma_count[eng] = n + 1
            slot = n % self.n_dma_sems
            o.sem = (eng, slot)
            o.semval = 16 * (n // self.n_dma_sems + 1)
            o.prev_same_sem = self.dma_last.get((eng, slot))
            self.dma_last[(eng, slot)] = o
        for r in reads:
            if dma:
                r.rd.append(o)
            else:
                r.rd = [x for x in r.rd if x.dma or x.eng != eng]
                r.rd.append(o)
        for w in writes:
            w.lw = o
            w.rd = []
        o.idx = len(self.ops[eng])
        self.ops[eng].append(o)
        return o

    def emit(self, block, sems, dma_sems):
        for e in ENGS:
            for o in self.ops[e]:
                for d in o.deps:
                    d.needed = True
        for e in ENGS:
            r = 0
            for o in self.ops[e]:
                if o.needed and not o.dma:
                    r += 1
                    o.rank = r
        trk = self

        def run(ename, eng):
            waited = {}
            for o in trk.ops[ename]:
                waits = []
                for d in o.deps:
                    if d.dma:
                        waits.append((dma_sems[d.sem], d.semval))
                    else:
                        waits.append((sems[d.eng], d.rank))
                if o.dma and o.prev_same_sem is not None:
                    p = o.prev_same_sem
                    waits.append((dma_sems[p.sem], p.semval))
                for s, v in waits:
                    k = s.num
                    if waited.get(k, 0) >= v:
                        continue
                    waited[k] = v
                    eng.wait_ge(s, v)
                if o.emit is None:
                    continue
                ins = o.emit(eng)
                if o.dma:
                    ins.then_inc(dma_sems[o.sem], 16)
                elif o.needed:
                    ins.then_inc(sems[ename], 1)

        @block.tensor
        def _(e):
            run("pe", e)

        @block.scalar
        def _(e):
            run("act", e)

        @block.vector
        def _(e):
            run("dve", e)

        @block.gpsimd
        def _(e):
            run("pool", e)

        @block.sync
        def _(e):
            run("sp", e)


class _StopBuild(Exception):
    pass


DBG_STOP = None
DBG_NPRE = None
DBG_NOTILES = False
import os as _os
DBG_SKIP = set((_os.environ.get('DBG_SKIP') or '').split(','))


def build_program(S, BP, BS):
    assert S % 512 == 0 and BS == 4
    NTS = BS * 64
    nc = bass.Bass("TRN2", target_bir_lowering=False)
    di = lambda n, s, d=F32: nc.dram_tensor(n, s, d, kind="ExternalInput")
    do = lambda n, s, d=F32: nc.dram_tensor(n, s, d, kind="ExternalOutput")
    xp = di("xp", [BP, S, D])
    xs = di("xs", [NTS, D])
    ck = di("ck", [L, BS, 128, 128])
    cv = di("cv", [L, BS, 128, 128])
    sc = di("sc", [L, BS, 30, 512])
    wsrc = di("wsrc", [L * NBLK, 128, 4096])
    gtab = di("gtab", [128, L * 2 * 8])
    cw = di("cw", [128, L * 4 * 31])
    cpar = di("cpar", [128, L * 3 * 4])
    fng = di("fng", [1, D])
    sink = di("sink", [1, L * 8])
    yp = do("yp", [BP, S, D])
    ys = do("ys", [NTS, D])
    nkp = do("nkp", [L, BP, 128, 128])
    nvp = do("nvp", [L, BP, 128, 128])
    ncp = do("ncp", [L, BP, 30, 512])
    nks = do("nks", [L, BS, 128, 128])
    nvs = do("nvs", [L, BS, 128, 128])
    ncs = do("ncs", [L, BS, 30, 512])
    wscr = nc.dram_tensor("wscr", [L * NBLK, 128, 4096], BF16)

    T = Tracker()
    es = ExitStack()
    with es:
        sb = lambda n, s, d: es.enter_context(nc.sbuf_tensor(n, s, d))
        pst = lambda n, s, d: es.enter_context(nc.psum_tensor(n, s, d))
        xb = [sb(f"xb{i}", [128, 4, 1024], F32) for i in range(2)]
        xn_tm = [sb(f"xn_tm{i}", [128, 1024], BF16) for i in range(2)]
        xnT = [sb(f"xnT{i}", [128, 8, 512], BF16) for i in range(2)]
        qT = sb("qT", [128, 4, 512], BF16)
        KT = [sb(f"KT{l}", [128, 768], BF16) for l in range(L)]
        VT = [sb(f"VT{l}", [128, 768], BF16) for l in range(L)]
        uT = [sb(f"uT{l}", [128, 4, 544], BF16) for l in range(L)]
        sig = [sb(f"sig{i}", [128, 512], F32) for i in range(2)]
        h_sb = sb("h_sb", [128, 4, 512], F32)
        lnA = sb("lnA", [128, 512], F32)
        lnB = sb("lnB", [128, 512], F32)
        lnC = sb("lnC", [128, 512], F32)
        lt2 = sb("lt2", [128, 512], F32)
        mixT = sb("mixT", [128, 8, 512], BF16)
        hidT = sb("hidT", [128, 32, 512], BF16)
        rl = [sb(f"rl{i}", [128, 512], BF16) for i in range(2)]
        PTw = [[sb(f"PTw{k}{b}", [128, 256], BF16) for b in range(2)] for k in range(2)]
        PTo = [[[sb(f"PTo{l}{k}{b}", [65, 256], BF16) for b in range(2)] for k in range(2)] for l in range(L)]
        Vw = [sb(f"Vw{i}", [128, 256], BF16) for i in range(2)]
        rcp = [sb(f"rcp{i}", [128, 256], F32) for i in range(2)]
        diag = [sb(f"diag{i}", [128, TP, 128], BF16) for i in range(2)]
        wslot = [sb(f"wslot{i}", [128, 4096], BF16) for i in range(4)]
        ident_f = sb("ident_f", [128, 128], F32)
        ident_b = sb("ident_b", [128, 128], BF16)
        ones512 = sb("ones512", [128, 128], BF16)
        ones1 = sb("ones1", [128, 64], BF16)
        gt = sb("gt", [128, 1024], F32)
        cw_sb = sb("cw_sb", [128, L * 4 * 31], F32)
        cpar_sb = sb("cpar_sb", [128, L * 3 * 4], F32)
        gtab_sb = sb("gtab_sb", [128, L * 2 * 8], F32)
        sink_sb = sb("sink_sb", [65, L * 8], F32)
        sinke_sb = sb("sinke_sb", [65, L * 8], F32)
        eps_sb = sb("eps_sb", [128, 1], F32)
        ss = sb("ss", [128, 4], F32)
        sst = sb("sst", [128, 4], F32)
        rstd = sb("rstd", [128, 4], F32)
        kf32 = sb("kf32", [128, 256], F32)
        vf32 = sb("vf32", [128, 256], F32)
        uf32 = sb("uf32", [128, 4, BS * 30], F32)
        ostg = [sb(f"ostg{i}", [128, 512], F32) for i in range(2)]
        ACC = [pst(f"ACC{i}", [128, 512], F32) for i in range(2)]
        TR = [pst(f"TR{i}", [128, 512], F32) for i in range(2)]
        TK = [pst(f"TK{i}", [128, 512], F32) for i in range(4)]
        TRb = [t.bitcast(BF16) for t in TR]
        R = {}

        def rs(name):
            if name not in R:
                R[name] = Res(name)
            return R[name]

        r_x = [[rs(f"x{i}_{tc}") for tc in range(4)] for i in range(2)]
        r_xn = [rs("xn0"), rs("xn1")]
        r_xnT = [rs("xnT0"), rs("xnT1")]
        r_xnTp = [[rs(f"xnT{w}p{h}") for h in range(2)] for w in range(2)]
        r_hid = [rs(f"hid{g}") for g in range(4)]
        r_ACC = [rs("ACC0"), rs("ACC1")]
        r_TR = [rs("TR0"), rs("TR1")]
        r_TK = [rs(f"TK{i}") for i in range(4)]
        r_slot = [rs(f"slot{i}") for i in range(4)]
        r_scr = [rs(f"scr{i}") for i in range(L * NBLK)]
        r_hsb = [rs(f"hsb{c}") for c in range(4)]
        r_ostg = [rs("ostg0"), rs("ostg1")]
        OUT = rs("OUT")

        sems = {e: es.enter_context(nc.semaphore("s_" + e)) for e in ENGS}
        dsems = {("sp", i): es.enter_context(nc.semaphore(f"d_sp{i}")) for i in range(T.n_dma_sems)}

        op = T.op

        def dma(out, in_, reads, writes):
            return op("sp", lambda e: e.dma_start(out=out, in_=in_), reads=reads, writes=writes, dma=True)

        op("pool", lambda e: e.memset(ident_f[:], 0.0), writes=[rs("ident_f")])
        op("pool", lambda e: e.affine_select(out=ident_f[:], in_=ident_f[:], pattern=[[-1, 128]],
                                             compare_op=ALU.not_equal, fill=1.0, base=0, channel_multiplier=1),
           reads=[rs("ident_f")], writes=[rs("ident_f")])
        op("pool", lambda e: e.tensor_copy(out=ident_b[:], in_=ident_f[:]), reads=[rs("ident_f")], writes=[rs("ident_b")])
        op("pool", lambda e: e.memset(ones512[:], 1.0 / 512.0), writes=[rs("ones512")])
        op("pool", lambda e: e.memset(ones1[:], 1.0), writes=[rs("ones1")])
        op("pool", lambda e: e.memset(eps_sb[:], EPS), writes=[rs("eps")])
        dma(gt[:], fng[0:1, :].partition_broadcast(128), [], [rs("gt")])
        dma(cw_sb[:], cw[:, :], [], [rs("cw")])
        dma(cpar_sb[:], cpar[:, :], [], [rs("cpar")])
        dma(gtab_sb[:], gtab[:, :], [], [rs("gtab")])
        dma(sink_sb[64:65, :], sink[0:1, :], [], [rs("sink")])
        op("act", lambda e: e.activation(out=sinke_sb[64:65, :], in_=sink_sb[64:65, :], func=AF.Exp),
           reads=[rs("sink")], writes=[rs("sinke")])
        for l in range(L):
            for k in range(2):
                for b in range(2):
                    src = sinke_sb[64:65, l * 8 + k * 4: l * 8 + k * 4 + 4].unsqueeze(2).broadcast_to([1, 4, 64])
                    dst = PTo[l][k][b][64:65, :].rearrange("p (g q) -> p g q", g=4)
                    op("pool", (lambda dst, src: lambda e: e.tensor_copy(out=dst, in_=src))(dst, src),
                       reads=[rs("sinke")], writes=[rs(f"PTo{l}{k}{b}")])

        class WS:
            n = 0
            issued = 0
            total = 0

        NW = L * NBLK
        cvt_rr = [0]
        stg = xb[1][:].rearrange("p a b -> p (a b)")
        r_stg = [rs("stgA"), rs("stgB")]

        def stage_load(blk, h):
            dma(stg[:, h * 2048:(h + 1) * 2048], wsrc[blk, :, h * 2048:(h + 1) * 2048], [], [r_stg[h]])

        def ws_issue():
            if WS.issued >= WS.total:
                return
            n = WS.issued
            WS.issued += 1
            blk = n % NW
            s_ = n % 4
            if n >= NW:
                dma(wslot[s_][:], wscr[blk, :, :], [r_scr[blk]], [r_slot[s_]])
                return
            l_, bl = divmod(blk, NBLK)
            ws_ = wslot[s_]
            scaled = None
            if bl < 4:
                scaled = l_ * 16 + 0
            elif 6 <= bl < 14:
                scaled = l_ * 16 + 8
            for h in range(2):
                for kc in range(4 * h, 4 * h + 4):
                    o_ = ws_[:, kc * 512:(kc + 1) * 512]
                    i_ = stg[:, kc * 512:(kc + 1) * 512]
                    eng = ("act", "dve")[cvt_rr[0] % 2]
                    cvt_rr[0] += 1
                    rd = [r_stg[h]]
                    if scaled is not None:
                        sc_ap = gtab_sb[:, scaled + kc: scaled + kc + 1]
                        rd.append(rs("gtab"))
                        if eng == "act":
                            f = (lambda o_, i_, s2: lambda e: e.activation(out=o_, in_=i_, func=AF.Copy, scale=s2))(o_, i_, sc_ap)
                        else:
                            f = (lambda o_, i_, s2: lambda e: e.tensor_scalar_mul(out=o_, in0=i_, scalar1=s2))(o_, i_, sc_ap)
                    else:
                        if eng == "act":
                            f = (lambda o_, i_: lambda e: e.activation(out=o_, in_=i_, func=AF.Copy))(o_, i_)
                        else:
                            f = (lambda o_, i_: lambda e: e.tensor_copy(out=o_, in_=i_))(o_, i_)
                    op(eng, f, reads=rd, writes=[r_slot[s_]])
                if blk + 1 < NW:
                    stage_load(blk + 1, h)
            dma(wscr[blk, :, :], ws_[:], [r_slot[s_]], [r_scr[blk]])

        def ws_get():
            s = WS.n % 4
            return wslot[s], r_slot[s]

        def ws_release():
            WS.n += 1
            ws_issue()

        tiles = []
        for b in range(BP):
            for ti in range(S // 512):
                tiles.append(("p", b, ti))
        tiles.append(("s", 0, 0))
        WS.total = len(tiles) * L * NBLK

        def x_load(idx):
            kind, b, ti = tiles[idx]
            xi = idx % 2
            if kind == "p":
                dma(xb[xi][:, :, :], xp[b, ti * 512:(ti + 1) * 512, :].rearrange("(tc p) d -> p tc d", p=128), [],
                    r_x[xi] + (r_stg if idx == 1 else []))
            else:
                dma(xb[xi][:, 0:2, :], xs[:, :].rearrange("(tc p) d -> p tc d", p=128), [], r_x[xi][0:2] + (r_stg if idx == 1 else []))

        acc_i = [0]
        tr_i = [0]

        def next_acc():
            a = acc_i[0] % 2
            acc_i[0] += 1
            return a

        def next_tr():
            a = tr_i[0] % 2
            tr_i[0] += 1
            return a

        rstd2 = sb("rstd2", [128, 4], F32)

        def norm_stats(xi, ntc, jsel):
            for tc in range(ntc):
                jk = xnT[jsel][:, 2 * tc:2 * tc + 2, :].rearrange("p a b -> p (a b)")
                op("act", (lambda tc, jk: lambda e: e.activation(out=jk, in_=xb[xi][:, tc, :], func=AF.Square,
                                                                 accum_out=ss[:, tc:tc + 1]))(tc, jk),
                   reads=[r_x[xi][tc]], writes=[r_xnT[jsel], rs(f"ss{tc}")])
            op("act", lambda e: e.activation(out=sst[:, 0:ntc], in_=ss[:, 0:ntc], func=AF.Ln, scale=1.0 / D, bias=eps_sb[:, 0:1]),
               reads=[rs(f"ss{t_}") for t_ in range(ntc)] + [rs("eps")], writes=[rs("sst")])
            op("act", lambda e: e.activation(out=rstd[:, 0:ntc], in_=sst[:, 0:ntc], func=AF.Exp, scale=-0.5),
               reads=[rs("sst")], writes=[rs("rstd")])

        def norm_cast(xi, tc, defer):
            xn = xn_tm[tc % 2]
            if defer:
                op("dve", (lambda tc, xn: lambda e: e.tensor_copy(out=xn[:], in_=xb[xi][:, tc, :]))(tc, xn),
                   reads=[r_x[xi][tc]], writes=[r_xn[tc % 2]])
            else:
                op("dve", (lambda tc, xn: lambda e: e.tensor_scalar_mul(out=xn[:], in0=xb[xi][:, tc, :],
                                                                        scalar1=rstd[:, tc:tc + 1]))(tc, xn),
                   reads=[r_x[xi][tc], rs("rstd")], writes=[r_xn[tc % 2]])

        def norm_tr(tc, which):
            xn = xn_tm[tc % 2]
            t = next_tr()
            for kc in range(8):
                op("pe", (lambda kc, xn, t: lambda e: e.transpose(out=TRb[t][:, kc * 128:(kc + 1) * 128],
                                                                  in_=xn[:, kc * 128:(kc + 1) * 128],
                                                                  identity=ident_b[:]))(kc, xn, t),
                   reads=[r_xn[tc % 2], rs("ident_b")], writes=[r_TR[t]])
            op("act", (lambda tc, t: lambda e: e.activation(
                out=xnT[which][:, :, tc * 128:(tc + 1) * 128],
                in_=TRb[t][:, :].rearrange("p (k c) -> p k c", k=8), func=AF.Copy))(tc, t),
               reads=[r_TR[t]], writes=[r_xnT[which], r_xnTp[which][tc // 2]])

        def norm_T(xi, ntc, which):
            norm_stats(xi, ntc, which)
            for tc in range(ntc):
                norm_cast(xi, tc, False)
                norm_tr(tc, which)

        def norm_defer_tail(xi, ntc):
            norm_stats(xi, ntc, 1)
            op("dve", lambda e: e.tensor_tensor(out=rstd2[:, 0:ntc], in0=rstd[:, 0:ntc], in1=rstd[:, 0:ntc], op=ALU.mult),
               reads=[rs("rstd")], writes=[rs("rstd2")])

        def layer(idx, l):
            kind, b, ti = tiles[idx]
            xi = idx % 2
            prompt = kind == "p"
            ntc = 4 if prompt else 2
            NT = ntc * 128
            nch = NT // 64
            first = prompt and ti == 0
            last = (not prompt) or ti == S // 512 - 1
            kt0 = 128 if prompt else 512

            def stage(n):
                if DBG_STOP is not None and (idx, l, n) == tuple(DBG_STOP):
                    raise _StopBuild()

            rKT, rVT, ruT = rs(f"KT{l}"), rs(f"VT{l}"), rs(f"uT{l}")
            cpo = l * 12

            if first:
                op("pool", lambda e: e.memset(KT[l][:, 0:128], 0.0), writes=[rKT])
                op("pool", lambda e: e.memset(VT[l][:, 0:128], 0.0), writes=[rVT])
                op("pool", lambda e: e.memset(uT[l][:, :, 0:30], 0.0), writes=[ruT])
            if not prompt:
                stK = hidT[:, 0:4, :].bitcast(F32).rearrange("p a b -> p (a b)")
                stV = hidT[:, 4:8, :].bitcast(F32).rearrange("p a b -> p (a b)")
                stC = hidT[:, 8:24, :].bitcast(F32).rearrange("p a b -> p (a b)")
                if l == 1:
                    stK = hidT[:, 24:28, :].bitcast(F32).rearrange("p a b -> p (a b)")
                    stV = hidT[:, 28:32, :].bitcast(F32).rearrange("p a b -> p (a b)")
                dma(stK[:, 0:512].rearrange("p (b f) -> p b f", b=BS), ck[l].rearrange("b t f -> t b f"), [], r_hid)
                dma(stV[:, 0:512].rearrange("p (b f) -> p b f", b=BS), cv[l].rearrange("b t f -> t b f"), [], r_hid)
                dma(stC[0:30, 0:2048].rearrange("p (b f) -> p b f", b=BS), sc[l].rearrange("b t f -> t b f"), [], r_hid)
                t = next_tr()
                for bb in range(BS):
                    op("pe", (lambda bb, t: lambda e: e.transpose(out=TR[t][:, bb * 128:(bb + 1) * 128],
                                                                  in_=stK[:, bb * 128:(bb + 1) * 128],
                                                                  identity=ident_f[:]))(bb, t),
                       reads=r_hid + [rs("ident_f")], writes=[r_TR[t]])
                op("act", (lambda t: lambda e: e.activation(out=KT[l][:, 0:512], in_=TR[t][:, :], func=AF.Copy))(t),
                   reads=[r_TR[t]], writes=[rKT])
                op("dve", lambda e: e.tensor_copy(out=VT[l][:, 0:512], in_=stV[:, 0:512]), reads=r_hid, writes=[rVT])
                t = next_tr()
                for cc in range(4):
                    for bb in range(BS):
                        c0 = (cc * BS + bb) * 30
                        op("pe", (lambda cc, bb, c0, t: lambda e: e.transpose(
                            out=TR[t][:, c0:c0 + 30], in_=stC[0:30, bb * 512 + cc * 128: bb * 512 + (cc + 1) * 128],
                            identity=ident_f[0:30, 0:30]))(cc, bb, c0, t),
                           reads=r_hid + [rs("ident_f")], writes=[r_TR[t]])
                op("act", (lambda t: lambda e: e.activation(
                    out=uT[l][:, :, 0:BS * 94].rearrange("p c (b t) -> p c b t", b=BS)[:, :, :, 0:30],
                    in_=TR[t][:, 0:480].rearrange("p (c b t) -> p c b t", c=4, b=BS), func=AF.Copy))(t),
                   reads=[r_TR[t]], writes=[ruT])

            def conv_build(cc):
                dg = diag[cc % 2]
                for j0 in range(0, TP, 8):
                    j1 = min(TP, j0 + 8)
                    wv = cw_sb[:, (l * 4 + cc) * 31 + j0:(l * 4 + cc) * 31 + j1]
                    op("pool", (lambda dg, wv, j0, j1: lambda e: e.tensor_tensor(
                        out=dg[:, j0:j1, :], in0=ident_b[:].unsqueeze(1).broadcast_to([128, j1 - j0, 128]),
                        in1=wv.unsqueeze(2).broadcast_to([128, j1 - j0, 128]), op=ALU.mult))(dg, wv, j0, j1),
                       reads=[rs("ident_b"), rs("cw")], writes=[rs(f"diag{cc % 2}")])

            stage(-1)
            norm_T(xi, ntc, 0)
            stage(0)
            conv_build(0)
            conv_build(1)

            mlist = ["q0", "q1", "q2", "q3", "k", "v", "g0", "a0", "g1", "a1", "g2", "a2", "g3", "a3"]
            for blk4 in range(4):
                slot, rslot = ws_get()
                for mi in range(4):
                    m = blk4 * 4 + mi
                    if m >= 14:
                        break
                    name = mlist[m]
                    if blk4 == 0 and mi == 0 and NT == 512:
                        pre = {0: next_acc(), 1: next_acc()}
                        for h in range(2):
                            for mi2 in (0, 1):
                                for kc in range(8):
                                    op("pe", (lambda a2, kc, mi2, slot, h: lambda e: e.matmul(
                                        ACC[a2][:, h * 256:(h + 1) * 256],
                                        lhsT=slot[:, kc * 512 + mi2 * 128: kc * 512 + (mi2 + 1) * 128],
                                        rhs=xnT[0][:, kc, h * 256:(h + 1) * 256], start=(kc == 0), stop=(kc == 7)))(pre[mi2], kc, mi2, slot, h),
                                       reads=[rslot, r_xnTp[0][h]], writes=[r_ACC[pre[mi2]]])
                    if blk4 == 0 and mi < 2 and NT == 512:
                        a = pre[mi]
                    else:
                        a = next_acc()
                        for kc in range(8):
                            op("pe", (lambda a, kc, mi, slot: lambda e: e.matmul(
                                ACC[a][:, 0:NT], lhsT=slot[:, kc * 512 + mi * 128: kc * 512 + (mi + 1) * 128],
                                rhs=xnT[0][:, kc, 0:NT], start=(kc == 0), stop=(kc == 7)))(a, kc, mi, slot),
                               reads=[rslot, r_xnT[0]], writes=[r_ACC[a]])
                    if name[0] in DBG_SKIP:
                        continue
                    if name[0] == "q":
                        j = int(name[1])
                        op("act", (lambda a, j: lambda e: e.activation(out=qT[:, j, 0:NT], in_=ACC[a][:, 0:NT],
                                                                       func=AF.Copy, scale=0.125))(a, j),
                           reads=[r_ACC[a]], writes=[rs("qT")])
                    elif name == "k":
                        if last:
                            c0 = NT - 128 if prompt else 0
                            w_ = 128 if prompt else 256
                            op("act", (lambda a, c0, w_: lambda e: e.activation(out=kf32[:, 0:w_], in_=ACC[a][:, c0:c0 + w_], func=AF.Copy))(a, c0, w_),
                               reads=[r_ACC[a]], writes=[rs("kf32")])
                        op("act", (lambda a: lambda e: e.activation(out=KT[l][:, kt0:kt0 + NT], in_=ACC[a][:, 0:NT],
                                                                    func=AF.Copy))(a),
                           reads=[r_ACC[a]], writes=[rKT])
                    elif name == "v":
                        if last:
                            c0 = NT - 128 if prompt else 0
                            w_ = 128 if prompt else 256
                            op("act", (lambda a, c0, w_: lambda e: e.activation(out=vf32[:, 0:w_], in_=ACC[a][:, c0:c0 + w_], func=AF.Copy))(a, c0, w_),
                               reads=[r_ACC[a]], writes=[rs("vf32")])
                        op("act", (lambda a: lambda e: e.activation(out=VT[l][:, kt0:kt0 + NT], in_=ACC[a][:, 0:NT], func=AF.Copy))(a),
                           reads=[r_ACC[a]], writes=[rVT])
                    elif name[0] == "g":
                        cc = int(name[1])
                        op("act", (lambda a, cc: lambda e: e.activation(out=sig[cc % 2][:, 0:NT], in_=ACC[a][:, 0:NT],
                                                                        func=AF.Sigmoid))(a, cc),
                           reads=[r_ACC[a]], writes=[rs(f"sig{cc % 2}")])
                    else:
                        cc = int(name[1])
                        if prompt:
                            o_ = uT[l][:, cc, 30:30 + NT]
                            i0 = ACC[a][:, 0:NT]
                            i1 = sig[cc % 2][:, 0:NT]
                        else:
                            o_ = uT[l][:, cc, 0:BS * 94].rearrange("p (b t) -> p b t", b=BS)[:, :, 30:94]
                            i0 = ACC[a][:, 0:NT].rearrange("p (b t) -> p b t", b=BS)
                            i1 = sig[cc % 2][:, 0:NT].rearrange("p (b t) -> p b t", b=BS)
                        op("dve", (lambda o_, i0, i1: lambda e: e.tensor_tensor(out=o_, in0=i0, in1=i1, op=ALU.mult))(o_, i0, i1),
                           reads=[r_ACC[a], rs(f"sig{cc % 2}")], writes=[ruT])
                        if last:
                            if prompt:
                                o2 = uf32[:, cc, 0:30]
                                j0 = ACC[a][:, NT - 30:NT]
                                j1 = sig[cc % 2][:, NT - 30:NT]
                            else:
                                o2 = uf32[:, cc, :].rearrange("p (b t) -> p b t", b=BS)
                                j0 = ACC[a][:, 0:NT].rearrange("p (b t) -> p b t", b=BS)[:, :, 34:64]
                                j1 = sig[cc % 2][:, 0:NT].rearrange("p (b t) -> p b t", b=BS)[:, :, 34:64]
                            op("dve", (lambda o2, j0, j1: lambda e: e.tensor_tensor(out=o2, in0=j0, in1=j1, op=ALU.mult))(o2, j0, j1),
                               reads=[r_ACC[a], rs(f"sig{cc % 2}")], writes=[rs("uf32")])
                ws_release()

            def u_tap(cc, j):
                if prompt:
                    return uT[l][:, cc, j:j + NT]
                return uT[l][:, cc, 0:BS * 94].rearrange("p (b t) -> p b t", b=BS)[:, :, j:j + 64]

            def conv_side_taps(cc):
                bcol = cpar_sb[:, cpo + cc: cpo + cc + 1]
                if prompt:
                    o_ = h_sb[:, cc, 0:NT]
                else:
                    o_ = h_sb[:, cc, 0:NT].rearrange("p (b t) -> p b t", b=BS)
                for j in range(TP, 31):
                    u_ = u_tap(cc, j)
                    w_ = cw_sb[:, (l * 4 + cc) * 31 + j:(l * 4 + cc) * 31 + j + 1]
                    if j == TP:
                        f = (lambda u_, w_: lambda e: e.tensor_scalar(out=o_, in0=u_, scalar1=w_, scalar2=bcol,
                                                                      op0=ALU.mult, op1=ALU.add))(u_, w_)
                    else:
                        f = (lambda u_, w_: lambda e: e.scalar_tensor_tensor(out=o_, in0=u_, scalar=w_, in1=o_,
                                                                             op0=ALU.mult, op1=ALU.add))(u_, w_)
                    op("dve", f, reads=[ruT, rs("cw"), rs("cpar"), r_hsb[cc]], writes=[r_hsb[cc]])

            for cc_ in range(4):
                conv_side_taps(cc_)

            stage(1)

            def attn_S(c):
                if "A" in DBG_SKIP:
                    return
                C = ti * 8 + c if prompt else 2
                bsel = c % 2
                if prompt:
                    KTwin = KT[l][:, c * 64: c * 64 + 128]
                    KTown = KT[l][:, 128 + c * 64: 128 + c * 64 + 64]
                else:
                    KTwin = KT[l][:, c * 128:(c + 1) * 128]
                    KTown = KT[l][:, 512 + c * 64: 512 + (c + 1) * 64]
                vw = Vw[bsel]
                rvw = rs(f"Vw{bsel}")
                t = next_tr()
                if prompt:
                    if C >= 1:
                        op("pe", (lambda t, c: lambda e: e.transpose(out=TRb[t][:, 0:128], in_=VT[l][:, c * 64: c * 64 + 128],
                                                                     identity=ident_b[:]))(t, c),
                           reads=[rVT, rs("ident_b")], writes=[r_TR[t]])
                    op("pe", (lambda t, c: lambda e: e.transpose(out=TRb[t][0:64, 128:256],
                                                                 in_=VT[l][:, 128 + c * 64: 128 + (c + 1) * 64],
                                                                 identity=ident_b[:]))(t, c),
                       reads=[rVT, rs("ident_b")], writes=[r_TR[t]])
                    if C >= 1:
                        op("act", (lambda t, vw: lambda e: e.activation(out=vw[:, 0:128], in_=TRb[t][:, 0:128], func=AF.Copy))(t, vw),
                           reads=[r_TR[t]], writes=[rvw])
                    op("act", (lambda t, vw: lambda e: e.activation(out=vw[0:64, 128:256], in_=TRb[t][0:64, 128:256], func=AF.Copy))(t, vw),
                       reads=[r_TR[t]], writes=[rvw])
                else:
                    op("pe", (lambda t, c: lambda e: e.transpose(out=TRb[t][0:64, 128:256],
                                                                 in_=VT[l][:, 512 + c * 64: 512 + (c + 1) * 64],
                                                                 identity=ident_b[:]))(t, c),
                       reads=[rVT, rs("ident_b")], writes=[r_TR[t]])
                    op("act", (lambda t, vw: lambda e: e.activation(out=vw[0:64, 128:256], in_=TRb[t][0:64, 128:256], func=AF.Copy))(t, vw),
                       reads=[r_TR[t]], writes=[rvw])
                for k in range(2):
                    rq = qT[64 * k:64 * k + 64, :, c * 64:(c + 1) * 64]
                    if C >= 1:
                        op("pe", (lambda k, rq, KTwin: lambda e: e.matmul(TK[k][:, 0:256], lhsT=KTwin[64 * k:64 * k + 64, :],
                                                                          rhs=rq, start=True, stop=True))(k, rq, KTwin),
                           reads=[rKT, rs("qT")], writes=[r_TK[k]])
                    op("pe", (lambda k, rq, KTown: lambda e: e.matmul(TK[k][0:64, 256:512], lhsT=KTown[64 * k:64 * k + 64, :],
                                                                      rhs=rq, start=True, stop=True))(k, rq, KTown),
                       reads=[rKT, rs("qT")], writes=[r_TK[k]])
                    if C >= 1:
                        op("act", (lambda k: lambda e: e.activation(out=PTw[k][bsel][:, :], in_=TK[k][:, 0:256], func=AF.Exp))(k),
                           reads=[r_TK[k]], writes=[rs(f"PTw{k}{bsel}")])
                    if C == 1:
                        op("pool", (lambda k: lambda e: e.memset(PTw[k][bsel][0:64, :], 0.0))(k),
                           reads=[rs(f"PTw{k}{bsel}")], writes=[rs(f"PTw{k}{bsel}")])
                    op("act", (lambda k: lambda e: e.activation(out=PTo[l][k][bsel][0:64, :], in_=TK[k][0:64, 256:512], func=AF.Exp))(k),
                       reads=[r_TK[k]], writes=[rs(f"PTo{l}{k}{bsel}")])

            def attn_PV(c):
                if "A" in DBG_SKIP or "P" in DBG_SKIP:
                    return
                C = ti * 8 + c if prompt else 2
                bsel = c % 2
                vw = Vw[bsel]
                rvw = rs(f"Vw{bsel}")
                od = 2 + bsel
                for k in range(2):
                    ptw, pto = PTw[k][bsel], PTo[l][k][bsel]
                    rptw, rpto = rs(f"PTw{k}{bsel}"), rs(f"PTo{l}{k}{bsel}")
                    orow = slice(64 * k, 64 * k + 64)
                    if C >= 1:
                        if prompt:
                            lw = vw[:, 64 * k:64 * k + 64]
                            rdv = [rvw]
                        else:
                            lw = VT[l][:, c * 128 + 64 * k: c * 128 + 64 * k + 64]
                            rdv = [rVT]
                        op("pe", (lambda lw, ptw, orow: lambda e: e.matmul(TK[od][orow, 0:256], lhsT=lw, rhs=ptw[:, :],
                                                                           start=True, stop=False))(lw, ptw, orow),
                           reads=rdv + [rptw], writes=[r_TK[od]])
                    op("pe", (lambda pto, orow, k: lambda e: e.matmul(TK[od][orow, 0:256], lhsT=vw[0:64, 128 + 64 * k:128 + 64 * k + 64],
                                                                      rhs=pto[0:64, :], start=(C == 0), stop=True))(pto, orow, k),
                       reads=[rvw, rpto], writes=[r_TK[od]])
                    if C >= 1:
                        op("pe", (lambda ptw, orow: lambda e: e.matmul(TK[od][orow, 256:512], lhsT=ones1[:, :], rhs=ptw[:, :],
                                                                       start=True, stop=False))(ptw, orow),
                           reads=[rs("ones1"), rptw], writes=[r_TK[od]])
                    op("pe", (lambda pto, orow: lambda e: e.matmul(TK[od][orow, 256:512], lhsT=ones1[0:65, :], rhs=pto[0:65, :],
                                                                   start=(C == 0), stop=True))(pto, orow),
                       reads=[rs("ones1"), rpto], writes=[r_TK[od]])
                if "N" in DBG_SKIP:
                    return
                op("dve", lambda e: e.reciprocal(out=rcp[bsel][:, :], in_=TK[od][:, 256:512]),
                   reads=[r_TK[od]], writes=[rs(f"rcp{bsel}")])
                op("dve", lambda e: e.tensor_tensor(out=mixT[:, 0:4, c * 64:(c + 1) * 64],
                                                    in0=TK[od][:, 0:256].rearrange("p (g q) -> p g q", g=4),
                                                    in1=rcp[bsel][:, :].rearrange("p (g q) -> p g q", g=4), op=ALU.mult),
                   reads=[r_TK[od], rs(f"rcp{bsel}")], writes=[rs(f"mixT_c{c}")])

            conv_state = {}

            def conv_mm(cc, j0, j1):
                if "C" in DBG_SKIP:
                    return
                dg = diag[cc % 2]
                a = next_acc()
                for j in range(TP):
                    op("pe", (lambda a, j, rhs, dg: lambda e: e.matmul(ACC[a][:, 0:NT], lhsT=dg[:, j, :], rhs=rhs,
                                                                       start=(j == 0), stop=(j == TP - 1)))(a, j, u_tap(cc, j), dg),
                       reads=[rs(f"diag{cc % 2}"), ruT], writes=[r_ACC[a]])
                op("dve", (lambda a, cc: lambda e: e.tensor_tensor(out=h_sb[:, cc, 0:NT], in0=ACC[a][:, 0:NT],
                                                                   in1=h_sb[:, cc, 0:NT], op=ALU.add))(a, cc),
                   reads=[r_ACC[a], r_hsb[cc]], writes=[r_hsb[cc]])
                op("act", (lambda cc: lambda e: e.activation(out=xnT[0][:, 4 + cc, 0:NT], in_=h_sb[:, cc, 0:NT],
                                                             func=AF.Square))(cc),
                   reads=[r_hsb[cc]], writes=[rs("hsq")])
                op("act", (lambda cc: lambda e: e.activation(out=xnT[0][:, cc, 0:NT], in_=h_sb[:, cc, 0:NT],
                                                             func=AF.Copy))(cc),
                   reads=[r_hsb[cc]], writes=[rs("hb")])

            R["hb"] = r_xnT[0]
            R["hsq"] = r_xnT[0]

            ln_st = {}

            def ln_stats():
                a_mean = next_acc()
                for cc in range(4):
                    op("pe", (lambda cc: lambda e: e.matmul(ACC[a_mean][:, 0:NT], lhsT=ones512[:, :], rhs=xnT[0][:, cc, 0:NT],
                                                            start=(cc == 0), stop=(cc == 3)))(cc),
                       reads=[rs("ones512"), r_xnT[0]], writes=[r_ACC[a_mean]])
                a_ex2 = next_acc()
                for cc in range(4):
                    op("pe", (lambda cc: lambda e: e.matmul(ACC[a_ex2][:, 0:NT], lhsT=ones512[:, :], rhs=xnT[0][:, 4 + cc, 0:NT],
                                                            start=(cc == 0), stop=(cc == 3)))(cc),
                       reads=[rs("ones512"), r_xnT[0]], writes=[r_ACC[a_ex2]])
                ln_st["m"], ln_st["e"] = a_mean, a_ex2

            def ln_a():
                a_mean, a_ex2 = ln_st["m"], ln_st["e"]
                op("act", lambda e: e.activation(out=lnA[:, 0:NT], in_=ACC[a_mean][:, 0:NT], func=AF.Copy),
                   reads=[r_ACC[a_mean]], writes=[rs("lnA")])
                op("dve", lambda e: e.tensor_tensor(out=lnB[:, 0:NT], in0=lnA[:, 0:NT], in1=lnA[:, 0:NT], op=ALU.mult),
                   reads=[rs("lnA")], writes=[rs("lnB")])
                op("dve", lambda e: e.tensor_tensor(out=lnB[:, 0:NT], in0=ACC[a_ex2][:, 0:NT], in1=lnB[:, 0:NT], op=ALU.subtract),
                   reads=[r_ACC[a_ex2], rs("lnB")], writes=[rs("lnB")])
                op("dve", lambda e: e.tensor_scalar_add(out=lnB[:, 0:NT], in0=lnB[:, 0:NT], scalar1=EPS),
                   reads=[rs("lnB")], writes=[rs("lnB")])

            def ln_b():
                op("act", lambda e: e.activation(out=lnB[:, 0:NT], in_=lnB[:, 0:NT], func=AF.Ln),
                   reads=[rs("lnB")], writes=[rs("lnB")])
                op("act", lambda e: e.activation(out=lnB[:, 0:NT], in_=lnB[:, 0:NT], func=AF.Exp, scale=-0.5),
                   reads=[rs("lnB")], writes=[rs("lnB")])

            def ln_c():
                op("dve", lambda e: e.scalar_tensor_tensor(out=lnC[:, 0:NT], in0=lnA[:, 0:NT], scalar=-1.0, in1=lnB[:, 0:NT],
                                                           op0=ALU.mult, op1=ALU.mult),
                   reads=[rs("lnA"), rs("lnB")], writes=[rs("lnC")])

            def lt_buf(cc):
                return [(sig[0], rs("sig0")), (sig[1], rs("sig1")), (lt2, rs("lt2")), (lnA, rs("lnA"))][cc]

            def ln_mul(cc):
                lt, rlt = lt_buf(cc)
                op("dve", (lambda cc, lt: lambda e: e.tensor_tensor(out=lt[:, 0:NT], in0=h_sb[:, cc, 0:NT], in1=lnB[:, 0:NT],
                                                                    op=ALU.mult))(cc, lt),
                   reads=[r_hsb[cc], rs("lnB")], writes=[rlt])
                op("pool", (lambda lt: lambda e: e.tensor_tensor(out=lt[:, 0:NT], in0=lt[:, 0:NT], in1=lnC[:, 0:NT], op=ALU.add))(lt),
                   reads=[rlt, rs("lnC")], writes=[rlt])

            def ln_silu(cc):
                lt, rlt = lt_buf(cc)
                op("act", (lambda cc, lt: lambda e: e.activation(out=mixT[:, 4 + cc, 0:NT], in_=lt[:, 0:NT], func=AF.Silu,
                                                                 scale=cpar_sb[:, cpo + 4 + cc: cpo + 5 + cc],
                                                                 bias=cpar_sb[:, cpo + 8 + cc: cpo + 9 + cc]))(cc, lt),
                   reads=[rlt, rs("cpar")], writes=[rs("mixTc")])

            attn_S(0)
            if prompt:
                for c in range(nch):
                    if c < 2:
                        conv_mm(2 * c, 0, 31)
                        conv_build(2 * c + 2) if 2 * c + 2 < 4 else None
                        attn_S(c + 1)
                        conv_mm(2 * c + 1, 0, 31)
                        conv_build(2 * c + 3) if 2 * c + 3 < 4 else None
                        if c == 1:
                            ln_stats()
                    elif c + 1 < nch:
                        attn_S(c + 1)
                    attn_PV(c)
                    if c == 2:
                        ln_a()
                    elif c == 3:
                        ln_b()
                    elif c == 4:
                        ln_c()
                        ln_mul(0)
                        ln_mul(1)
                    elif c == 5:
                        ln_mul(2)
                        ln_mul(3)
                    elif c == 6:
                        for cc_ in range(4):
                            ln_silu(cc_)
            else:
                for c in range(nch):
                    if c < 2:
                        for cc in (2 * c, 2 * c + 1):
                            conv_mm(cc, 0, 31)
                            if cc + 2 < 4:
                                conv_build(cc + 2)
                        if c == 1:
                            ln_stats()
                    if c + 1 < nch:
                        attn_S(c + 1)
                    attn_PV(c)
                    if c == 2:
                        ln_a()
                        ln_b()
                    elif c == 3:
                        ln_c()
                        ln_mul(0)
                        ln_mul(1)
                ln_mul(2)
                ln_mul(3)
                for cc_ in range(4):
                    ln_silu(cc_)

            stage(2)
            if prompt and not last:
                op("pool", lambda e: e.tensor_copy(out=KT[l][:, 0:128], in_=KT[l][:, 512:640]), reads=[rKT], writes=[rKT])
                op("pool", lambda e: e.tensor_copy(out=VT[l][:, 0:128], in_=VT[l][:, 512:640]), reads=[rVT], writes=[rVT])
                op("pool", lambda e: e.tensor_copy(out=uT[l][:, :, 0:30], in_=uT[l][:, :, 512:542]), reads=[ruT], writes=[ruT])

            stage(3)
            for half in range(2):
                slot, rslot = ws_get()

                def wo_mm(tc, kcs):
                    for kc in kcs:
                        rd_mix = [rs(f"mixT_c{2 * tc}"), rs(f"mixT_c{2 * tc + 1}")] if kc < 4 else [rs("mixTc")]
                        op("pe", (lambda tc, kc, slot: lambda e: e.matmul(
                            TK[tc][:, :], lhsT=mixT[:, kc, tc * 128:(tc + 1) * 128], rhs=slot[:, kc * 512:(kc + 1) * 512],
                            start=(kc == 0), stop=(kc == 7)))(tc, kc, slot),
                           reads=[rslot] + rd_mix, writes=[r_TK[tc]])

                if half == 0:
                    for tc in range(ntc):
                        wo_mm(tc, range(0, 4))
                for tc in range(ntc):
                    wo_mm(tc, range(4, 8) if half == 0 else range(8))
                    xs_ = xb[xi][:, tc, half * 512:(half + 1) * 512]
                    op("dve", (lambda tc, xs_: lambda e: e.tensor_tensor(out=xs_, in0=TK[tc][:, :], in1=xs_, op=ALU.add))(tc, xs_),
                       reads=[r_TK[tc], r_x[xi][tc]], writes=[r_x[xi][tc]])
                    if half == 1:
                        norm_cast(xi, tc, True)
                        if tc >= 1:
                            norm_tr(tc - 1, 1)
                ws_release()
            norm_tr(ntc - 1, 1)

            stage(4)
            if last:
                if prompt:
                    for (src, rsrc, dst) in ((kf32, rs("kf32"), nkp), (vf32, rs("vf32"), nvp)):
                        t = next_tr()
                        og = next_tr()
                        op("pe", (lambda t, src: lambda e: e.transpose(out=TR[t][:, 0:128], in_=src[:, 0:128], identity=ident_f[:]))(t, src),
                           reads=[rsrc, rs("ident_f")], writes=[r_TR[t]])
                        op("act", (lambda t, og: lambda e: e.activation(out=ostg[og][:, 0:128], in_=TR[t][:, 0:128], func=AF.Copy))(t, og),
                           reads=[r_TR[t]], writes=[r_ostg[og]])
                        dma(dst[l, b, :, :], ostg[og][:, 0:128], [r_ostg[og], OUT], [])
                    t = next_tr()
                    og = next_tr()
                    for cc in range(4):
                        op("pe", (lambda t, cc: lambda e: e.transpose(out=TR[t][0:30, cc * 128:(cc + 1) * 128], in_=uf32[:, cc, 0:30],
                                                                      identity=ident_f[:]))(t, cc),
                           reads=[rs("uf32"), rs("ident_f")], writes=[r_TR[t]])
                    op("act", (lambda t, og: lambda e: e.activation(out=ostg[og][0:30, :], in_=TR[t][0:30, :], func=AF.Copy))(t, og),
                       reads=[r_TR[t]], writes=[r_ostg[og]])
                    dma(ncp[l, b, :, :], ostg[og][0:30, :], [r_ostg[og], OUT], [])
                else:
                    for (src, rsrc, dst, cache) in ((kf32, rs("kf32"), nks, ck), (vf32, rs("vf32"), nvs, cv)):
                        t = next_tr()
                        og = next_tr()
                        for bb in range(BS):
                            op("pe", (lambda t, bb, src: lambda e: e.transpose(out=TR[t][0:64, bb * 128:(bb + 1) * 128],
                                                                               in_=src[:, bb * 64:(bb + 1) * 64], identity=ident_f[:]))(t, bb, src),
                               reads=[rsrc, rs("ident_f")], writes=[r_TR[t]])
                        op("act", (lambda t, og: lambda e: e.activation(out=ostg[og][0:64, :], in_=TR[t][0:64, :], func=AF.Copy))(t, og),
                           reads=[r_TR[t]], writes=[r_ostg[og]])
                        dma(dst[l, :, 64:128, :].rearrange("b t f -> t b f"), ostg[og][0:64, :].rearrange("p (b f) -> p b f", b=BS),
                            [r_ostg[og], OUT], [])
                        dma(dst[l, :, 0:64, :], cache[l, :, 64:128, :], [OUT], [])
                    for bb in range(BS):
                        t = next_tr()
                        og = next_tr()
                        for cc in range(4):
                            op("pe", (lambda t, cc, bb: lambda e: e.transpose(out=TR[t][0:30, cc * 128:(cc + 1) * 128],
                                                                              in_=uf32[:, cc, bb * 30:(bb + 1) * 30],
                                                                              identity=ident_f[:]))(t, cc, bb),
                               reads=[rs("uf32"), rs("ident_f")], writes=[r_TR[t]])
                        op("act", (lambda t, og: lambda e: e.activation(out=ostg[og][0:30, :], in_=TR[t][0:30, :], func=AF.Copy))(t, og),
                           reads=[r_TR[t]], writes=[r_ostg[og]])
                        dma(ncs[l, bb, :, :], ostg[og][0:30, :], [r_ostg[og], OUT], [])

            for blk8 in range(8):
                slot, rslot = ws_get()
                for mi in range(4):
                    m = blk8 * 4 + mi
                    if blk8 == 0 and mi == 0 and NT == 512:
                        pre = {0: next_acc(), 1: next_acc()}
                        for h in range(2):
                            for mi2 in (0, 1):
                                for kc in range(8):
                                    op("pe", (lambda a2, kc, mi2, slot, h: lambda e: e.matmul(
                                        ACC[a2][:, h * 256:(h + 1) * 256],
                                        lhsT=slot[:, kc * 512 + mi2 * 128: kc * 512 + (mi2 + 1) * 128],
                                        rhs=xnT[1][:, kc, h * 256:(h + 1) * 256], start=(kc == 0), stop=(kc == 7)))(pre[mi2], kc, mi2, slot, h),
                                       reads=[rslot, r_xnTp[1][h]], writes=[r_ACC[pre[mi2]]])
                    if blk8 == 0 and mi < 2 and NT == 512:
                        a = pre[mi]
                    else:
                        a = next_acc()
                        for kc in range(8):
                            op("pe", (lambda a, kc, mi, slot: lambda e: e.matmul(
                                ACC[a][:, 0:NT], lhsT=slot[:, kc * 512 + mi * 128: kc * 512 + (mi + 1) * 128],
                                rhs=xnT[1][:, kc, 0:NT], start=(kc == 0), stop=(kc == 7)))(a, kc, mi, slot),
                               reads=[rslot, r_xnT[1]], writes=[r_ACC[a]])
                    op("act", (lambda a, m: lambda e: e.activation(out=rl[m % 2][:, 0:NT], in_=ACC[a][:, 0:NT], func=AF.Relu))(a, m),
                       reads=[r_ACC[a]], writes=[rs(f"rl{m % 2}")])
                    op("dve", (lambda m: lambda e: e.tensor_tensor(out=hidT[:, m, 0:NT], in0=rl[m % 2][:, 0:NT], in1=rl[m % 2][:, 0:NT],
                                                                   op=ALU.mult))(m),
                       reads=[rs(f"rl{m % 2}")], writes=[r_hid[m // 8]])
                ws_release()

            stage(5)
            norm_defer_tail(xi, ntc)
            for half in range(2):
                for kg in range(4):
                    slot, rslot = ws_get()
                    for tc in range(ntc):
                        for kcl in range(8):
                            kc = kg * 8 + kcl
                            op("pe", (lambda tc, kc, kcl, slot: lambda e: e.matmul(
                                TK[tc][:, :], lhsT=hidT[:, kc, tc * 128:(tc + 1) * 128], rhs=slot[:, kcl * 512:(kcl + 1) * 512],
                                start=(kc == 0), stop=(kc == 31)))(tc, kc, kcl, slot),
                               reads=[rslot, r_hid[kg]], writes=[r_TK[tc]])
                    ws_release()
                for tc in range(ntc):
                    xs_ = xb[xi][:, tc, half * 512:(half + 1) * 512]
                    op("dve", (lambda tc, xs_: lambda e: e.scalar_tensor_tensor(out=xs_, in0=TK[tc][:, :], scalar=rstd2[:, tc:tc + 1],
                                                                                in1=xs_, op0=ALU.mult, op1=ALU.add))(tc, xs_),
                       reads=[r_TK[tc], r_x[xi][tc], rs("rstd2")], writes=[r_x[xi][tc]])

        def final_norm(idx):
            kind, b, ti = tiles[idx]
            xi = idx % 2
            ntc = 4 if kind == "p" else 2
            norm_stats(xi, ntc, 1)
            for tc in range(ntc):
                op("dve", (lambda tc: lambda e: e.scalar_tensor_tensor(out=xb[xi][:, tc, :], in0=xb[xi][:, tc, :],
                                                                       scalar=rstd[:, tc:tc + 1], in1=gt[:, :],
                                                                       op0=ALU.mult, op1=ALU.mult))(tc),
                   reads=[r_x[xi][tc], rs("rstd"), rs("gt")], writes=[r_x[xi][tc]])
            if kind == "p":
                dma(yp[b, ti * 512:(ti + 1) * 512, :].rearrange("(tc p) d -> p tc d", p=128), xb[xi][:, :, :], r_x[xi] + [OUT], [])
            else:
                dma(ys[:, :].rearrange("(tc p) d -> p tc d", p=128), xb[xi][:, 0:2, :], r_x[xi][0:2] + [OUT], [])

        if not DBG_NOTILES:
            x_load(0)
            stage_load(0, 0)
            stage_load(0, 1)
            for _ in range(4):
                ws_issue()
        try:
            if DBG_NOTILES:
                raise _StopBuild()
            for idx in range(len(tiles)):
                if idx >= 1 and idx + 1 < len(tiles):
                    x_load(idx + 1)
                for l in range(L):
                    layer(idx, l)
                if idx == 0 and len(tiles) > 1:
                    x_load(1)
                final_norm(idx)
        except _StopBuild:
            pass
        op("sp", None, writes=[OUT] + list(R.values()))

        with nc.Block() as block:
            T.emit(block, sems, dsems)
    return nc


def _prep_weights(norm1, w_in, conv_w, conv_b, conv_ln_g, conv_ln_b, w_out, norm2, w_up, w_down):
    wsrc = np.empty((L * NBLK, 128, 4096), np.float32)
    qcols = [np.r_[j * 64:(j + 1) * 64, (j + 4) * 64:(j + 5) * 64] for j in range(4)]
    mcols = qcols + [np.arange(512, 640), np.arange(640, 768)]
    for cc in range(4):
        mcols.append(np.arange(1280 + cc * 128, 1280 + (cc + 1) * 128))
        mcols.append(np.arange(768 + cc * 128, 768 + (cc + 1) * 128))
    orow = np.concatenate([np.r_[j * 64:(j + 1) * 64, (j + 4) * 64:(j + 5) * 64] for j in range(4)] + [np.arange(512, 1024)])
    for l in range(L):
        base = l * NBLK
        wi = w_in[l].reshape(8, 128, 1792)
        for blk in range(4):
            buf = np.zeros((128, 8, 4, 128), np.float32)
            for mi in range(4):
                m = blk * 4 + mi
                if m < 14:
                    buf[:, :, mi, :] = wi[:, :, mcols[m]].transpose(1, 0, 2)
            wsrc[base + blk] = buf.reshape(128, 4096)
        wo = w_out[l][orow].reshape(8, 128, 1024)
        for half in range(2):
            wsrc[base + 4 + half] = wo[:, :, half * 512:(half + 1) * 512].transpose(1, 0, 2).reshape(128, 4096)
        wu = w_up[l].reshape(8, 128, 32, 128)
        for blk in range(8):
            wsrc[base + 6 + blk] = wu[:, :, blk * 4:(blk + 1) * 4, :].transpose(1, 0, 2, 3).reshape(128, 4096)
        wd = w_down[l].reshape(4, 8, 128, 1024)
        for half in range(2):
            for kg in range(4):
                wsrc[base + 14 + half * 4 + kg] = wd[kg][:, :, half * 512:(half + 1) * 512].transpose(1, 0, 2).reshape(128, 4096)
    gtab = np.empty((128, L * 16), np.float32)
    cw = np.empty((128, L * 4 * 31), np.float32)
    cpar = np.empty((128, L * 12), np.float32)
    for l in range(L):
        gtab[:, l * 16:l * 16 + 8] = norm1[l].reshape(8, 128).T
        gtab[:, l * 16 + 8:l * 16 + 16] = norm2[l].reshape(8, 128).T
        cw[:, l * 124:(l + 1) * 124] = conv_w[l].T.reshape(4, 128, 31).transpose(1, 0, 2).reshape(128, 124)
        cpar[:, l * 12:l * 12 + 4] = conv_b[l].reshape(4, 128).T
        cpar[:, l * 12 + 4:l * 12 + 8] = conv_ln_g[l].reshape(4, 128).T
        cpar[:, l * 12 + 8:l * 12 + 12] = conv_ln_b[l].reshape(4, 128).T
    return wsrc, gtab, cw, cpar


_PROG_CACHE = {}


def kernel(x_prompt, x_sample, cache_k, cache_v, state_conv, norm1, w_in, attn_sink, conv_w,
           conv_b, conv_ln_g, conv_ln_b, w_out, norm2, w_up, w_down, final_norm):
    f = lambda a: np.ascontiguousarray(np.asarray(a, dtype=np.float32))
    x_prompt, x_sample, cache_k, cache_v, state_conv = map(f, (x_prompt, x_sample, cache_k, cache_v, state_conv))
    B, S, _ = x_prompt.shape
    DB, TS, _ = x_sample.shape
    BP, BS = B // NCORES, DB // NCORES
    assert TS == 64
    wsrc, gtab, cw, cpar = _prep_weights(*map(f, (norm1, w_in, conv_w, conv_b, conv_ln_g, conv_ln_b, w_out, norm2, w_up, w_down)))
    fng = f(final_norm).reshape(1, D)
    sink = f(attn_sink).reshape(1, L * 8)
    key = (S, BP, BS)
    if key not in _PROG_CACHE:
        _PROG_CACHE[key] = build_program(S, BP, BS)
    nc = _PROG_CACHE[key]
    win = cache_k.shape[2]
    in_maps = []
    for c in range(NCORES):
        in_maps.append({
            "xp": x_prompt[c * BP:(c + 1) * BP],
            "xs": x_sample[c * BS:(c + 1) * BS].reshape(BS * 64, D),
            "ck": np.ascontiguousarray(cache_k[:, c * BS:(c + 1) * BS].reshape(L, BS, win, 128)),
            "cv": np.ascontiguousarray(cache_v[:, c * BS:(c + 1) * BS].reshape(L, BS, win, 128)),
            "sc": np.ascontiguousarray(state_conv[:, c * BS:(c + 1) * BS]),
            "wsrc": wsrc, "gtab": gtab, "cw": cw, "cpar": cpar, "fng": fng, "sink": sink,
        })
    res = run_bass_kernel_spmd(nc, in_maps, core_ids=list(range(NCORES)))
    rr = res.results
    cat = lambda name, axis: np.concatenate([np.asarray(r[name]) for r in rr], axis=axis)
    y_prompt = cat("yp", 0)
    y_sample = cat("ys", 0).reshape(DB, TS, D)
    nkp = cat("nkp", 1).reshape(L, B, 128, 2, 64)
    nvp = cat("nvp", 1).reshape(L, B, 128, 2, 64)
    ncp = cat("ncp", 1)
    nks = cat("nks", 1).reshape(L, DB, 128, 2, 64)
    nvs = cat("nvs", 1).reshape(L, DB, 128, 2, 64)
    ncs = cat("ncs", 1)
    return (y_prompt, y_sample, nkp, nvp, ncp, nks, nvs, ncs)
```

```python
import numpy as np
from contextlib import ExitStack
import concourse.bass as bass
import concourse.mybir as mybir
from concourse.bass_utils import run_bass_kernel_spmd

F32 = mybir.dt.float32
BF16 = mybir.dt.bfloat16
AF = mybir.ActivationFunctionType
ALU = mybir.AluOpType

ENGS = ("pe", "act", "dve", "pool", "sp")
NCORES = 8
L = 2
D = 1024
NBLK = 22
EPS = 1e-5
TP, TD = 24, 7
STRICT_SAME_ENGINE = True


class Res:
    __slots__ = ("name", "lw", "rd")

    def __init__(self, name):
        self.name = name
        self.lw = None
        self.rd = []


class Op:
    __slots__ = ("eng", "emit", "deps", "idx", "needed", "dma", "sem", "semval", "rank", "prev_same_sem")

    def __init__(self, eng, emit, dma):
        self.eng = eng
        self.emit = emit
        self.deps = []
        self.needed = False
        self.dma = dma
        self.sem = None
        self.semval = 0
        self.rank = 0
        self.prev_same_sem = None
        self.idx = 0


class Tracker:
    def __init__(self, n_dma_sems=12):
        self.ops = {e: [] for e in ENGS}
        self.n_dma_sems = n_dma_sems
        self.dma_count = {e: 0 for e in ENGS}
        self.dma_last = {}

    def op(self, eng, emit, reads=(), writes=(), dma=False):
        o = Op(eng, emit, dma)
        deps = {}
        for r in reads:
            if r.lw is not None:
                deps[id(r.lw)] = (r.lw, "raw")
        for w in writes:
            if w.lw is not None and id(w.lw) not in deps:
                deps[id(w.lw)] = (w.lw, "waw")
            for rr in w.rd:
                if id(rr) not in deps:
                    deps[id(rr)] = (rr, "war")
        best = {}
        for d, kind in deps.values():
            if (not d.dma) and d.eng == eng and not dma:
                if eng == "pe" or (kind != "raw" and not STRICT_SAME_ENGINE):
                    continue
            if d.dma:
                o.deps.append(d)
            else:
                b = best.get(d.eng)
                if b is None or d.idx > b.idx:
                    best[d.eng] = d
        o.deps.extend(best.values())
        if dma:
            n = self.dma_count[eng]
            self.dma_count[eng] = n + 1
            slot = n % self.n_dma_sems
            o.sem = (eng, slot)
            o.semval = 16 * (n // self.n_dma_sems + 1)
            o.prev_same_sem = self.dma_last.get((eng, slot))
            self.dma_last[(eng, slot)] = o
        for r in reads:
            if dma:
                r.rd.append(o)
            else:
                r.rd = [x for x in r.rd if x.dma or x.eng != eng]
                r.rd.append(o)
        for w in writes:
            w.lw = o
            w.rd = []
        o.idx = len(self.ops[eng])
        self.ops[eng].append(o)
        return o

    def emit(self, block, sems, dma_sems):
        for e in ENGS:
            for o in self.ops[e]:
                for d in o.deps:
                    d.needed = True
        for e in ENGS:
            r = 0
            for o in self.ops[e]:
                if o.needed and not o.dma:
                    r += 1
                    o.rank = r
        trk = self

        def run(ename, eng):
            waited = {}
            for o in trk.ops[ename]:
                waits = []
                for d in o.deps:
                    if d.dma:
                        waits.append((dma_sems[d.sem], d.semval))
                    else:
                        waits.append((sems[d.eng], d.rank))
                if o.dma and o.prev_same_sem is not None:
                    p = o.prev_same_sem
                    waits.append((dma_sems[p.sem], p.semval))
                for s, v in waits:
                    k = s.num
                    if waited.get(k, 0) >= v:
                        continue
                    waited[k] = v
                    eng.wait_ge(s, v)
                if o.emit is None:
                    continue
                ins = o.emit(eng)
                if o.dma:
                    ins.then_inc(dma_sems[o.sem], 16)
                elif o.needed:
                    ins.then_inc(sems[ename], 1)

        @block.tensor
        def _(e):
            run("pe", e)

        @block.scalar
        def _(e):
            run("act", e)

        @block.vector
        def _(e):
            run("dve", e)

        @block.gpsimd
        def _(e):
            run("pool", e)

        @block.sync
        def _(e):
            run("sp", e)


class _StopBuild(Exception):
    pass


DBG_STOP = None
DBG_NPRE = None
DBG_NOTILES = False
import os as _os
DBG_SKIP = set((_os.environ.get('DBG_SKIP') or '').split(','))


def build_program(S, BP, BS):
    assert S % 512 == 0 and BS == 4
    NTS = BS * 64
    nc = bass.Bass("TRN2", target_bir_lowering=False)
    di = lambda n, s, d=F32: nc.dram_tensor(n, s, d, kind="ExternalInput")
    do = lambda n, s, d=F32: nc.dram_tensor(n, s, d, kind="ExternalOutput")
    xp = di("xp", [BP, S, D])
    xs = di("xs", [NTS, D])
    ck = di("ck", [L, BS, 128, 128])
    cv = di("cv", [L, BS, 128, 128])
    sc = di("sc", [L, BS, 30, 512])
    wsrc = di("wsrc", [L * NBLK, 128, 4096])
    gtab = di("gtab", [128, L * 2 * 8])
    cw = di("cw", [128, L * 4 * 31])
    cpar = di("cpar", [128, L * 3 * 4])
    fng = di("fng", [1, D])
    sink = di("sink", [1, L * 8])
    yp = do("yp", [BP, S, D])
    ys = do("ys", [NTS, D])
    nkp = do("nkp", [L, BP, 128, 128])
    nvp = do("nvp", [L, BP, 128, 128])
    ncp = do("ncp", [L, BP, 30, 512])
    nks = do("nks", [L, BS, 128, 128])
    nvs = do("nvs", [L, BS, 128, 128])
    ncs = do("ncs", [L, BS, 30, 512])
    wscr = nc.dram_tensor("wscr", [L * NBLK, 128, 4096], BF16)

    T = Tracker()
    es = ExitStack()
    with es:
        sb = lambda n, s, d: es.enter_context(nc.sbuf_tensor(n, s, d))
        pst = lambda n, s, d: es.enter_context(nc.psum_tensor(n, s, d))
        xb = [sb(f"xb{i}", [128, 4, 1024], F32) for i in range(2)]
        xn_tm = [sb(f"xn_tm{i}", [128, 1024], BF16) for i in range(2)]
        xnT = [sb(f"xnT{i}", [128, 8, 512], BF16) for i in range(2)]
        qT = sb("qT", [128, 4, 512], BF16)
        KT = [sb(f"KT{l}", [128, 768], BF16) for l in range(L)]
        VT = [sb(f"VT{l}", [128, 768], BF16) for l in range(L)]
        uT = [sb(f"uT{l}", [128, 4, 544], BF16) for l in range(L)]
        sig = [sb(f"sig{i}", [128, 512], F32) for i in range(2)]
        h_sb = sb("h_sb", [128, 4, 512], F32)
        lnA = sb("lnA", [128, 512], F32)
        lnB = sb("lnB", [128, 512], F32)
        lnC = sb("lnC", [128, 512], F32)
        lt2 = sb("lt2", [128, 512], F32)
        mixT = sb("mixT", [128, 8, 512], BF16)
        hidT = sb("hidT", [128, 32, 512], BF16)
        rl = [sb(f"rl{i}", [128, 512], BF16) for i in range(2)]
        PTw = [[sb(f"PTw{k}{b}", [128, 256], BF16) for b in range(2)] for k in range(2)]
        PTo = [[[sb(f"PTo{l}{k}{b}", [65, 256], BF16) for b in range(2)] for k in range(2)] for l in range(L)]
        Vw = [sb(f"Vw{i}", [128, 256], BF16) for i in range(2)]
        rcp = [sb(f"rcp{i}", [128, 256], F32) for i in range(2)]
        diag = [sb(f"diag{i}", [128, TP, 128], BF16) for i in range(2)]
        wslot = [sb(f"wslot{i}", [128, 4096], BF16) for i in range(4)]
        ident_f = sb("ident_f", [128, 128], F32)
        ident_b = sb("ident_b", [128, 128], BF16)
        ones512 = sb("ones512", [128, 128], BF16)
        ones1 = sb("ones1", [128, 64], BF16)
        gt = sb("gt", [128, 1024], F32)
        cw_sb = sb("cw_sb", [128, L * 4 * 31], F32)
        cpar_sb = sb("cpar_sb", [128, L * 3 * 4], F32)
        gtab_sb = sb("gtab_sb", [128, L * 2 * 8], F32)
        sink_sb = sb("sink_sb", [65, L * 8], F32)
        sinke_sb = sb("sinke_sb", [65, L * 8], F32)
        eps_sb = sb("eps_sb", [128, 1], F32)
        ss = sb("ss", [128, 4], F32)
        sst = sb("sst", [128, 4], F32)
        rstd = sb("rstd", [128, 4], F32)
        kf32 = sb("kf32", [128, 256], F32)
        vf32 = sb("vf32", [128, 256], F32)
        uf32 = sb("uf32", [128, 4, BS * 30], F32)
        ostg = [sb(f"ostg{i}", [128, 512], F32) for i in range(2)]
        ACC = [pst(f"ACC{i}", [128, 512], F32) for i in range(2)]
        TR = [pst(f"TR{i}", [128, 512], F32) for i in range(2)]
        TK = [pst(f"TK{i}", [128, 512], F32) for i in range(4)]
        TRb = [t.bitcast(BF16) for t in TR]
        R = {}

        def rs(name):
            if name not in R:
                R[name] = Res(name)
            return R[name]

        r_x = [[rs(f"x{i}_{tc}") for tc in range(4)] for i in range(2)]
        r_xn = [rs("xn0"), rs("xn1")]
        r_xnT = [rs("xnT0"), rs("xnT1")]
        r_xnTp = [[rs(f"xnT{w}p{h}") for h in range(2)] for w in range(2)]
        r_hid = [rs(f"hid{g}") for g in range(4)]
        r_ACC = [rs("ACC0"), rs("ACC1")]
        r_TR = [rs("TR0"), rs("TR1")]
        r_TK = [rs(f"TK{i}") for i in range(4)]
        r_slot = [rs(f"slot{i}") for i in range(4)]
        r_scr = [rs(f"scr{i}") for i in range(L * NBLK)]
        r_hsb = [rs(f"hsb{c}") for c in range(4)]
        r_ostg = [rs("ostg0"), rs("ostg1")]
        OUT = rs("OUT")

        sems = {e: es.enter_context(nc.semaphore("s_" + e)) for e in ENGS}
        dsems = {("sp", i): es.enter_context(nc.semaphore(f"d_sp{i}")) for i in range(T.n_dma_sems)}

        op = T.op

        def dma(out, in_, reads, writes):
            return op("sp", lambda e: e.dma_start(out=out, in_=in_), reads=reads, writes=writes, dma=True)

        op("pool", lambda e: e.memset(ident_f[:], 0.0), writes=[rs("ident_f")])
        op("pool", lambda e: e.affine_select(out=ident_f[:], in_=ident_f[:], pattern=[[-1, 128]],
                                             compare_op=ALU.not_equal, fill=1.0, base=0, channel_multiplier=1),
           reads=[rs("ident_f")], writes=[rs("ident_f")])
        op("pool", lambda e: e.tensor_copy(out=ident_b[:], in_=ident_f[:]), reads=[rs("ident_f")], writes=[rs("ident_b")])
        op("pool", lambda e: e.memset(ones512[:], 1.0 / 512.0), writes=[rs("ones512")])
        op("pool", lambda e: e.memset(ones1[:], 1.0), writes=[rs("ones1")])
        op("pool", lambda e: e.memset(eps_sb[:], EPS), writes=[rs("eps")])
        dma(gt[:], fng[0:1, :].partition_broadcast(128), [], [rs("gt")])
        dma(cw_sb[:], cw[:, :], [], [rs("cw")])
        dma(cpar_sb[:], cpar[:, :], [], [rs("cpar")])
        dma(gtab_sb[:], gtab[:, :], [], [rs("gtab")])
        dma(sink_sb[64:65, :], sink[0:1, :], [], [rs("sink")])
        op("act", lambda e: e.activation(out=sinke_sb[64:65, :], in_=sink_sb[64:65, :], func=AF.Exp),
           reads=[rs("sink")], writes=[rs("sinke")])
        for l in range(L):
            for k in range(2):
                for b in range(2):
                    src = sinke_sb[64:65, l * 8 + k * 4: l * 8 + k * 4 + 4].unsqueeze(2).broadcast_to([1, 4, 64])
                    dst = PTo[l][k][b][64:65, :].rearrange("p (g q) -> p g q", g=4)
                    op("pool", (lambda dst, src: lambda e: e.tensor_copy(out=dst, in_=src))(dst, src),
                       reads=[rs("sinke")], writes=[rs(f"PTo{l}{k}{b}")])

        class WS:
            n = 0
            issued = 0
            total = 0

        NW = L * NBLK
        cvt_rr = [0]
        stg = xb[1][:].rearrange("p a b -> p (a b)")
        r_stg = [rs("stgA"), rs("stgB")]

        def stage_load(blk, h):
            dma(stg[:, h * 2048:(h + 1) * 2048], wsrc[blk, :, h * 2048:(h + 1) * 2048], [], [r_stg[h]])

        def ws_issue():
            if WS.issued >= WS.total:
                return
            n = WS.issued
            WS.issued += 1
            blk = n % NW
            s_ = n % 4
            if n >= NW:
                dma(wslot[s_][:], wscr[blk, :, :], [r_scr[blk]], [r_slot[s_]])
                return
            l_, bl = divmod(blk, NBLK)
            ws_ = wslot[s_]
            scaled = None
            if bl < 4:
                scaled = l_ * 16 + 0
            elif 6 <= bl < 14:
                scaled = l_ * 16 + 8
            for h in range(2):
                for kc in range(4 * h, 4 * h + 4):
                    o_ = ws_[:, kc * 512:(kc + 1) * 512]
                    i_ = stg[:, kc * 512:(kc + 1) * 512]
                    eng = ("act", "dve")[cvt_rr[0] % 2]
                    cvt_rr[0] += 1
                    rd = [r_stg[h]]
                    if scaled is not None:
                        sc_ap = gtab_sb[:, scaled + kc: scaled + kc + 1]
                        rd.append(rs("gtab"))
                        if eng == "act":
                            f = (lambda o_, i_, s2: lambda e: e.activation(out=o_, in_=i_, func=AF.Copy, scale=s2))(o_, i_, sc_ap)
                        else:
                            f = (lambda o_, i_, s2: lambda e: e.tensor_scalar_mul(out=o_, in0=i_, scalar1=s2))(o_, i_, sc_ap)
                    else:
                        if eng == "act":
                            f = (lambda o_, i_: lambda e: e.activation(out=o_, in_=i_, func=AF.Copy))(o_, i_)
                        else:
                            f = (lambda o_, i_: lambda e: e.tensor_copy(out=o_, in_=i_))(o_, i_)
                    op(eng, f, reads=rd, writes=[r_slot[s_]])
                if blk + 1 < NW:
                    stage_load(blk + 1, h)
            dma(wscr[blk, :, :], ws_[:], [r_slot[s_]], [r_scr[blk]])

        def ws_get():
            s = WS.n % 4
            return wslot[s], r_slot[s]

        def ws_release():
            WS.n += 1
            ws_issue()

        tiles = []
        for b in range(BP):
            for ti in range(S // 512):
                tiles.append(("p", b, ti))
        tiles.append(("s", 0, 0))
        WS.total = len(tiles) * L * NBLK

        def x_load(idx):
            kind, b, ti = tiles[idx]
            xi = idx % 2
            if kind == "p":
                dma(xb[xi][:, :, :], xp[b, ti * 512:(ti + 1) * 512, :].rearrange("(tc p) d -> p tc d", p=128), [],
                    r_x[xi] + (r_stg if idx == 1 else []))
            else:
                dma(xb[xi][:, 0:2, :], xs[:, :].rearrange("(tc p) d -> p tc d", p=128), [], r_x[xi][0:2] + (r_stg if idx == 1 else []))

        acc_i = [0]
        tr_i = [0]

        def next_acc():
            a = acc_i[0] % 2
            acc_i[0] += 1
            return a

        def next_tr():
            a = tr_i[0] % 2
            tr_i[0] += 1
            return a

        rstd2 = sb("rstd2", [128, 4], F32)

        def norm_stats(xi, ntc, jsel):
            for tc in range(ntc):
                jk = xnT[jsel][:, 2 * tc:2 * tc + 2, :].rearrange("p a b -> p (a b)")
                op("act", (lambda tc, jk: lambda e: e.activation(out=jk, in_=xb[xi][:, tc, :], func=AF.Square,
                                                                 accum_out=ss[:, tc:tc + 1]))(tc, jk),
                   reads=[r_x[xi][tc]], writes=[r_xnT[jsel], rs(f"ss{tc}")])
            op("act", lambda e: e.activation(out=sst[:, 0:ntc], in_=ss[:, 0:ntc], func=AF.Ln, scale=1.0 / D, bias=eps_sb[:, 0:1]),
               reads=[rs(f"ss{t_}") for t_ in range(ntc)] + [rs("eps")], writes=[rs("sst")])
            op("act", lambda e: e.activation(out=rstd[:, 0:ntc], in_=sst[:, 0:ntc], func=AF.Exp, scale=-0.5),
               reads=[rs("sst")], writes=[rs("rstd")])

        def norm_cast(xi, tc, defer):
            xn = xn_tm[tc % 2]
            if defer:
                op("dve", (lambda tc, xn: lambda e: e.tensor_copy(out=xn[:], in_=xb[xi][:, tc, :]))(tc, xn),
                   reads=[r_x[xi][tc]], writes=[r_xn[tc % 2]])
            else:
                op("dve", (lambda tc, xn: lambda e: e.tensor_scalar_mul(out=xn[:], in0=xb[xi][:, tc, :],
                                                                        scalar1=rstd[:, tc:tc + 1]))(tc, xn),
                   reads=[r_x[xi][tc], rs("rstd")], writes=[r_xn[tc % 2]])

        def norm_tr(tc, which):
            xn = xn_tm[tc % 2]
            t = next_tr()
            for kc in range(8):
                op("pe", (lambda kc, xn, t: lambda e: e.transpose(out=TRb[t][:, kc * 128:(kc + 1) * 128],
                                                                  in_=xn[:, kc * 128:(kc + 1) * 128],
                                                                  identity=ident_b[:]))(kc, xn, t),
                   reads=[r_xn[tc % 2], rs("ident_b")], writes=[r_TR[t]])
            op("act", (lambda tc, t: lambda e: e.activation(
                out=xnT[which][:, :, tc * 128:(tc + 1) * 128],
                in_=TRb[t][:, :].rearrange("p (k c) -> p k c", k=8), func=AF.Copy))(tc, t),
               reads=[r_TR[t]], writes=[r_xnT[which], r_xnTp[which][tc // 2]])

        def norm_T(xi, ntc, which):
            norm_stats(xi, ntc, which)
            for tc in range(ntc):
                norm_cast(xi, tc, False)
                norm_tr(tc, which)

        def norm_defer_tail(xi, ntc):
            norm_stats(xi, ntc, 1)
            op("dve", lambda e: e.tensor_tensor(out=rstd2[:, 0:ntc], in0=rstd[:, 0:ntc], in1=rstd[:, 0:ntc], op=ALU.mult),
               reads=[rs("rstd")], writes=[rs("rstd2")])

        def layer(idx, l):
            kind, b, ti = tiles[idx]
            xi = idx % 2
            prompt = kind == "p"
            ntc = 4 if prompt else 2
            NT = ntc * 128
            nch = NT // 64
            first = prompt and ti == 0
            last = (not prompt) or ti == S // 512 - 1
            kt0 = 128 if prompt else 512

            def stage(n):
                if DBG_STOP is not None and (idx, l, n) == tuple(DBG_STOP):
                    raise _StopBuild()

            rKT, rVT, ruT = rs(f"KT{l}"), rs(f"VT{l}"), rs(f"uT{l}")
            cpo = l * 12

            if first:
                op("pool", lambda e: e.memset(KT[l][:, 0:128], 0.0), writes=[rKT])
                op("pool", lambda e: e.memset(VT[l][:, 0:128], 0.0), writes=[rVT])
                op("pool", lambda e: e.memset(uT[l][:, :, 0:30], 0.0), writes=[ruT])
            if not prompt:
                stK = hidT[:, 0:4, :].bitcast(F32).rearrange("p a b -> p (a b)")
                stV = hidT[:, 4:8, :].bitcast(F32).rearrange("p a b -> p (a b)")
                stC = hidT[:, 8:24, :].bitcast(F32).rearrange("p a b -> p (a b)")
                if l == 1:
                    stK = hidT[:, 24:28, :].bitcast(F32).rearrange("p a b -> p (a b)")
                    stV = hidT[:, 28:32, :].bitcast(F32).rearrange("p a b -> p (a b)")
                dma(stK[:, 0:512].rearrange("p (b f) -> p b f", b=BS), ck[l].rearrange("b t f -> t b f"), [], r_hid)
                dma(stV[:, 0:512].rearrange("p (b f) -> p b f", b=BS), cv[l].rearrange("b t f -> t b f"), [], r_hid)
                dma(stC[0:30, 0:2048].rearrange("p (b f) -> p b f", b=BS), sc[l].rearrange("b t f -> t b f"), [], r_hid)
                t = next_tr()
                for bb in range(BS):
                    op("pe", (lambda bb, t: lambda e: e.transpose(out=TR[t][:, bb * 128:(bb + 1) * 128],
                                                                  in_=stK[:, bb * 128:(bb + 1) * 128],
                                                                  identity=ident_f[:]))(bb, t),
                       reads=r_hid + [rs("ident_f")], writes=[r_TR[t]])
                op("act", (lambda t: lambda e: e.activation(out=KT[l][:, 0:512], in_=TR[t][:, :], func=AF.Copy))(t),
                   reads=[r_TR[t]], writes=[rKT])
                op("dve", lambda e: e.tensor_copy(out=VT[l][:, 0:512], in_=stV[:, 0:512]), reads=r_hid, writes=[rVT])
                t = next_tr()
                for cc in range(4):
                    for bb in range(BS):
                        c0 = (cc * BS + bb) * 30
                        op("pe", (lambda cc, bb, c0, t: lambda e: e.transpose(
                            out=TR[t][:, c0:c0 + 30], in_=stC[0:30, bb * 512 + cc * 128: bb * 512 + (cc + 1) * 128],
                            identity=ident_f[0:30, 0:30]))(cc, bb, c0, t),
                           reads=r_hid + [rs("ident_f")], writes=[r_TR[t]])
                op("act", (lambda t: lambda e: e.activation(
                    out=uT[l][:, :, 0:BS * 94].rearrange("p c (b t) -> p c b t", b=BS)[:, :, :, 0:30],
                    in_=TR[t][:, 0:480].rearrange("p (c b t) -> p c b t", c=4, b=BS), func=AF.Copy))(t),
                   reads=[r_TR[t]], writes=[ruT])

            def conv_build(cc):
                dg = diag[cc % 2]
                for j0 in range(0, TP, 8):
                    j1 = min(TP, j0 + 8)
                    wv = cw_sb[:, (l * 4 + cc) * 31 + j0:(l * 4 + cc) * 31 + j1]
                    op("pool", (lambda dg, wv, j0, j1: lambda e: e.tensor_tensor(
                        out=dg[:, j0:j1, :], in0=ident_b[:].unsqueeze(1).broadcast_to([128, j1 - j0, 128]),
                        in1=wv.unsqueeze(2).broadcast_to([128, j1 - j0, 128]), op=ALU.mult))(dg, wv, j0, j1),
                       reads=[rs("ident_b"), rs("cw")], writes=[rs(f"diag{cc % 2}")])

            stage(-1)
            norm_T(xi, ntc, 0)
            stage(0)
            conv_build(0)
            conv_build(1)

            mlist = ["q0", "q1", "q2", "q3", "k", "v", "g0", "a0", "g1", "a1", "g2", "a2", "g3", "a3"]
            for blk4 in range(4):
                slot, rslot = ws_get()
                for mi in range(4):
                    m = blk4 * 4 + mi
                    if m >= 14:
                        break
                    name = mlist[m]
                    if blk4 == 0 and mi == 0 and NT == 512:
                        pre = {0: next_acc(), 1: next_acc()}
                        for h in range(2):
                            for mi2 in (0, 1):
                                for kc in range(8):
                                    op("pe", (lambda a2, kc, mi2, slot, h: lambda e: e.matmul(
                                        ACC[a2][:, h * 256:(h + 1) * 256],
                                        lhsT=slot[:, kc * 512 + mi2 * 128: kc * 512 + (mi2 + 1) * 128],
                                        rhs=xnT[0][:, kc, h * 256:(h + 1) * 256], start=(kc == 0), stop=(kc == 7)))(pre[mi2], kc, mi2, slot, h),
                                       reads=[rslot, r_xnTp[0][h]], writes=[r_ACC[pre[mi2]]])
                    if blk4 == 0 and mi < 2 and NT == 512:
                        a = pre[mi]
                    else:
                        a = next_acc()
                        for kc in range(8):
                            op("pe", (lambda a, kc, mi, slot: lambda e: e.matmul(
                                ACC[a][:, 0:NT], lhsT=slot[:, kc * 512 + mi * 128: kc * 512 + (mi + 1) * 128],
                                rhs=xnT[0][:, kc, 0:NT], start=(kc == 0), stop=(kc == 7)))(a, kc, mi, slot),
                               reads=[rslot, r_xnT[0]], writes=[r_ACC[a]])
                    if name[0] in DBG_SKIP:
                        continue
                    if name[0] == "q":
                        j = int(name[1])
                        op("act", (lambda a, j: lambda e: e.activation(out=qT[:, j, 0:NT], in_=ACC[a][:, 0:NT],
                                                                       func=AF.Copy, scale=0.125))(a, j),
                           reads=[r_ACC[a]], writes=[rs("qT")])
                    elif name == "k":
                        if last:
                            c0 = NT - 128 if prompt else 0
                            w_ = 128 if prompt else 256
                            op("act", (lambda a, c0, w_: lambda e: e.activation(out=kf32[:, 0:w_], in_=ACC[a][:, c0:c0 + w_], func=AF.Copy))(a, c0, w_),
                               reads=[r_ACC[a]], writes=[rs("kf32")])
                        op("act", (lambda a: lambda e: e.activation(out=KT[l][:, kt0:kt0 + NT], in_=ACC[a][:, 0:NT],
                                                                    func=AF.Copy))(a),
                           reads=[r_ACC[a]], writes=[rKT])
                    elif name == "v":
                        if last:
                            c0 = NT - 128 if prompt else 0
                            w_ = 128 if prompt else 256
                            op("act", (lambda a, c0, w_: lambda e: e.activation(out=vf32[:, 0:w_], in_=ACC[a][:, c0:c0 + w_], func=AF.Copy))(a, c0, w_),
                               reads=[r_ACC[a]], writes=[rs("vf32")])
                        op("act", (lambda a: lambda e: e.activation(out=VT[l][:, kt0:kt0 + NT], in_=ACC[a][:, 0:NT], func=AF.Copy))(a),
                           reads=[r_ACC[a]], writes=[rVT])
                    elif name[0] == "g":
                        cc = int(name[1])
                        op("act", (lambda a, cc: lambda e: e.activation(out=sig[cc % 2][:, 0:NT], in_=ACC[a][:, 0:NT],
                                                                        func=AF.Sigmoid))(a, cc),
                           reads=[r_ACC[a]], writes=[rs(f"sig{cc % 2}")])
                    else:
                        cc = int(name[1])
                        if prompt:
                            o_ = uT[l][:, cc, 30:30 + NT]
                            i0 = ACC[a][:, 0:NT]
                            i1 = sig[cc % 2][:, 0:NT]
                        else:
                            o_ = uT[l][:, cc, 0:BS * 94].rearrange("p (b t) -> p b t", b=BS)[:, :, 30:94]
                            i0 = ACC[a][:, 0:NT].rearrange("p (b t) -> p b t", b=BS)
                            i1 = sig[cc % 2][:, 0:NT].rearrange("p (b t) -> p b t", b=BS)
                        op("dve", (lambda o_, i0, i1: lambda e: e.tensor_tensor(out=o_, in0=i0, in1=i1, op=ALU.mult))(o_, i0, i1),
                           reads=[r_ACC[a], rs(f"sig{cc % 2}")], writes=[ruT])
                        if last:
                            if prompt:
                                o2 = uf32[:, cc, 0:30]
                                j0 = ACC[a][:, NT - 30:NT]
                                j1 = sig[cc % 2][:, NT - 30:NT]
                            else:
                                o2 = uf32[:, cc, :].rearrange("p (b t) -> p b t", b=BS)
                                j0 = ACC[a][:, 0:NT].rearrange("p (b t) -> p b t", b=BS)[:, :, 34:64]
                                j1 = sig[cc % 2][:, 0:NT].rearrange("p (b t) -> p b t", b=BS)[:, :, 34:64]
                            op("dve", (lambda o2, j0, j1: lambda e: e.tensor_tensor(out=o2, in0=j0, in1=j1, op=ALU.mult))(o2, j0, j1),
                               reads=[r_ACC[a], rs(f"sig{cc % 2}")], writes=[rs("uf32")])
                ws_release()

            def u_tap(cc, j):
                if prompt:
                    return uT[l][:, cc, j:j + NT]
                return uT[l][:, cc, 0:BS * 94].rearrange("p (b t) -> p b t", b=BS)[:, :, j:j + 64]

            def conv_side_taps(cc):
                bcol = cpar_sb[:, cpo + cc: cpo + cc + 1]
                if prompt:
                    o_ = h_sb[:, cc, 0:NT]
                else:
                    o_ = h_sb[:, cc, 0:NT].rearrange("p (b t) -> p b t", b=BS)
                for j in range(TP, 31):
                    u_ = u_tap(cc, j)
                    w_ = cw_sb[:, (l * 4 + cc) * 31 + j:(l * 4 + cc) * 31 + j + 1]
                    if j == TP:
                        f = (lambda u_, w_: lambda e: e.tensor_scalar(out=o_, in0=u_, scalar1=w_, scalar2=bcol,
                                                                      op0=ALU.mult, op1=ALU.add))(u_, w_)
                    else:
                        f = (lambda u_, w_: lambda e: e.scalar_tensor_tensor(out=o_, in0=u_, scalar=w_, in1=o_,
                                                                             op0=ALU.mult, op1=ALU.add))(u_, w_)
                    op("dve", f, reads=[ruT, rs("cw"), rs("cpar"), r_hsb[cc]], writes=[r_hsb[cc]])

            stage(1)

            def attn_S(c):
                if "A" in DBG_SKIP:
                    return
                C = ti * 8 + c if prompt else 2
                bsel = c % 2
                if prompt:
                    KTwin = KT[l][:, c * 64: c * 64 + 128]
                    KTown = KT[l][:, 128 + c * 64: 128 + c * 64 + 64]
                else:
                    KTwin = KT[l][:, c * 128:(c + 1) * 128]
                    KTown = KT[l][:, 512 + c * 64: 512 + (c + 1) * 64]
                vw = Vw[bsel]
                rvw = rs(f"Vw{bsel}")
                t = next_tr()
                if prompt:
                    if C >= 1:
                        op("pe", (lambda t, c: lambda e: e.transpose(out=TRb[t][:, 0:128], in_=VT[l][:, c * 64: c * 64 + 128],
                                                                     identity=ident_b[:]))(t, c),
                           reads=[rVT, rs("ident_b")], writes=[r_TR[t]])
                    op("pe", (lambda t, c: lambda e: e.transpose(out=TRb[t][0:64, 128:256],
                                                                 in_=VT[l][:, 128 + c * 64: 128 + (c + 1) * 64],
                                                                 identity=ident_b[:]))(t, c),
                       reads=[rVT, rs("ident_b")], writes=[r_TR[t]])
                    if C >= 1:
                        op("act", (lambda t, vw: lambda e: e.activation(out=vw[:, 0:128], in_=TRb[t][:, 0:128], func=AF.Copy))(t, vw),
                           reads=[r_TR[t]], writes=[rvw])
                    op("act", (lambda t, vw: lambda e: e.activation(out=vw[0:64, 128:256], in_=TRb[t][0:64, 128:256], func=AF.Copy))(t, vw),
                       reads=[r_TR[t]], writes=[rvw])
                else:
                    op("pe", (lambda t, c: lambda e: e.transpose(out=TRb[t][0:64, 128:256],
                                                                 in_=VT[l][:, 512 + c * 64: 512 + (c + 1) * 64],
                                                                 identity=ident_b[:]))(t, c),
                       reads=[rVT, rs("ident_b")], writes=[r_TR[t]])
                    op("act", (lambda t, vw: lambda e: e.activation(out=vw[0:64, 128:256], in_=TRb[t][0:64, 128:256], func=AF.Copy))(t, vw),
                       reads=[r_TR[t]], writes=[rvw])
                for k in range(2):
                    rq = qT[64 * k:64 * k + 64, :, c * 64:(c + 1) * 64]
                    if C >= 1:
                        op("pe", (lambda k, rq, KTwin: lambda e: e.matmul(TK[k][:, 0:256], lhsT=KTwin[64 * k:64 * k + 64, :],
                                                                          rhs=rq, start=True, stop=True))(k, rq, KTwin),
                           reads=[rKT, rs("qT")], writes=[r_TK[k]])
                    op("pe", (lambda k, rq, KTown: lambda e: e.matmul(TK[k][0:64, 256:512], lhsT=KTown[64 * k:64 * k + 64, :],
                                                                      rhs=rq, start=True, stop=True))(k, rq, KTown),
                       reads=[rKT, rs("qT")], writes=[r_TK[k]])
                    if C >= 1:
                        op("act", (lambda k: lambda e: e.activation(out=PTw[k][bsel][:, :], in_=TK[k][:, 0:256], func=AF.Exp))(k),
                           reads=[r_TK[k]], writes=[rs(f"PTw{k}{bsel}")])
                    if C == 1:
                        op("pool", (lambda k: lambda e: e.memset(PTw[k][bsel][0:64, :], 0.0))(k),
                           reads=[rs(f"PTw{k}{bsel}")], writes=[rs(f"PTw{k}{bsel}")])
                    op("act", (lambda k: lambda e: e.activation(out=PTo[l][k][bsel][0:64, :], in_=TK[k][0:64, 256:512], func=AF.Exp))(k),
                       reads=[r_TK[k]], writes=[rs(f"PTo{l}{k}{bsel}")])

            def attn_PV(c):
                if "A" in DBG_SKIP or "P" in DBG_SKIP:
                    return
                C = ti * 8 + c if prompt else 2
                bsel = c % 2
                vw = Vw[bsel]
                rvw = rs(f"Vw{bsel}")
                od = 2 + bsel
                for k in range(2):
                    ptw, pto = PTw[k][bsel], PTo[l][k][bsel]
                    rptw, rpto = rs(f"PTw{k}{bsel}"), rs(f"PTo{l}{k}{bsel}")
                    orow = slice(64 * k, 64 * k + 64)
                    if C >= 1:
                        if prompt:
                            lw = vw[:, 64 * k:64 * k + 64]
                            rdv = [rvw]
                        else:
                            lw = VT[l][:, c * 128 + 64 * k: c * 128 + 64 * k + 64]
                            rdv = [rVT]
                        op("pe", (lambda lw, ptw, orow: lambda e: e.matmul(TK[od][orow, 0:256], lhsT=lw, rhs=ptw[:, :],
                                                                           start=True, stop=False))(lw, ptw, orow),
                           reads=rdv + [rptw], writes=[r_TK[od]])
                    op("pe", (lambda pto, orow, k: lambda e: e.matmul(TK[od][orow, 0:256], lhsT=vw[0:64, 128 + 64 * k:128 + 64 * k + 64],
                                                                      rhs=pto[0:64, :], start=(C == 0), stop=True))(pto, orow, k),
                       reads=[rvw, rpto], writes=[r_TK[od]])
                    if C >= 1:
                        op("pe", (lambda ptw, orow: lambda e: e.matmul(TK[od][orow, 256:512], lhsT=ones1[:, :], rhs=ptw[:, :],
                                                                       start=True, stop=False))(ptw, orow),
                           reads=[rs("ones1"), rptw], writes=[r_TK[od]])
                    op("pe", (lambda pto, orow: lambda e: e.matmul(TK[od][orow, 256:512], lhsT=ones1[0:65, :], rhs=pto[0:65, :],
                                                                   start=(C == 0), stop=True))(pto, orow),
                       reads=[rs("ones1"), rpto], writes=[r_TK[od]])
                if "N" in DBG_SKIP:
                    return
                op("dve", lambda e: e.reciprocal(out=rcp[bsel][:, :], in_=TK[od][:, 256:512]),
                   reads=[r_TK[od]], writes=[rs(f"rcp{bsel}")])
                op("dve", lambda e: e.tensor_tensor(out=mixT[:, 0:4, c * 64:(c + 1) * 64],
                                                    in0=TK[od][:, 0:256].rearrange("p (g q) -> p g q", g=4),
                                                    in1=rcp[bsel][:, :].rearrange("p (g q) -> p g q", g=4), op=ALU.mult),
                   reads=[r_TK[od], rs(f"rcp{bsel}")], writes=[rs(f"mixT_c{c}")])

            conv_state = {}

            def conv_mm(cc, j0, j1):
                if "C" in DBG_SKIP:
                    return
                dg = diag[cc % 2]
                conv_side_taps(cc)
                a = next_acc()
                for j in range(TP):
                    op("pe", (lambda a, j, rhs, dg: lambda e: e.matmul(ACC[a][:, 0:NT], lhsT=dg[:, j, :], rhs=rhs,
                                                                       start=(j == 0), stop=(j == TP - 1)))(a, j, u_tap(cc, j), dg),
                       reads=[rs(f"diag{cc % 2}"), ruT], writes=[r_ACC[a]])
                op("dve", (lambda a, cc: lambda e: e.tensor_tensor(out=h_sb[:, cc, 0:NT], in0=ACC[a][:, 0:NT],
                                                                   in1=h_sb[:, cc, 0:NT], op=ALU.add))(a, cc),
                   reads=[r_ACC[a], r_hsb[cc]], writes=[r_hsb[cc]])
                op("act", (lambda cc: lambda e: e.activation(out=xnT[0][:, 4 + cc, 0:NT], in_=h_sb[:, cc, 0:NT],
                                                             func=AF.Square))(cc),
                   reads=[r_hsb[cc]], writes=[rs("hsq")])
                op("act", (lambda cc: lambda e: e.activation(out=xnT[0][:, cc, 0:NT], in_=h_sb[:, cc, 0:NT],
                                                             func=AF.Copy))(cc),
                   reads=[r_hsb[cc]], writes=[rs("hb")])

            R["hb"] = r_xnT[0]
            R["hsq"] = r_xnT[0]

            ln_st = {}

            def ln_stats():
                a_mean = next_acc()
                for cc in range(4):
                    op("pe", (lambda cc: lambda e: e.matmul(ACC[a_mean][:, 0:NT], lhsT=ones512[:, :], rhs=xnT[0][:, cc, 0:NT],
                                                            start=(cc == 0), stop=(cc == 3)))(cc),
                       reads=[rs("ones512"), r_xnT[0]], writes=[r_ACC[a_mean]])
                a_ex2 = next_acc()
                for cc in range(4):
                    op("pe", (lambda cc: lambda e: e.matmul(ACC[a_ex2][:, 0:NT], lhsT=ones512[:, :], rhs=xnT[0][:, 4 + cc, 0:NT],
                                                            start=(cc == 0), stop=(cc == 3)))(cc),
                       reads=[rs("ones512"), r_xnT[0]], writes=[r_ACC[a_ex2]])
                ln_st["m"], ln_st["e"] = a_mean, a_ex2

            def ln_a():
                a_mean, a_ex2 = ln_st["m"], ln_st["e"]
                op("act", lambda e: e.activation(out=lnA[:, 0:NT], in_=ACC[a_mean][:, 0:NT], func=AF.Copy),
                   reads=[r_ACC[a_mean]], writes=[rs("lnA")])
                op("dve", lambda e: e.tensor_tensor(out=lnB[:, 0:NT], in0=lnA[:, 0:NT], in1=lnA[:, 0:NT], op=ALU.mult),
                   reads=[rs("lnA")], writes=[rs("lnB")])
                op("dve", lambda e: e.tensor_tensor(out=lnB[:, 0:NT], in0=ACC[a_ex2][:, 0:NT], in1=lnB[:, 0:NT], op=ALU.subtract),
                   reads=[r_ACC[a_ex2], rs("lnB")], writes=[rs("lnB")])
                op("dve", lambda e: e.tensor_scalar_add(out=lnB[:, 0:NT], in0=lnB[:, 0:NT], scalar1=EPS),
                   reads=[rs("lnB")], writes=[rs("lnB")])

            def ln_b():
                op("act", lambda e: e.activation(out=lnB[:, 0:NT], in_=lnB[:, 0:NT], func=AF.Ln),
                   reads=[rs("lnB")], writes=[rs("lnB")])
                op("act", lambda e: e.activation(out=lnB[:, 0:NT], in_=lnB[:, 0:NT], func=AF.Exp, scale=-0.5),
                   reads=[rs("lnB")], writes=[rs("lnB")])

            def ln_c():
                op("dve", lambda e: e.scalar_tensor_tensor(out=lnC[:, 0:NT], in0=lnA[:, 0:NT], scalar=-1.0, in1=lnB[:, 0:NT],
                                                           op0=ALU.mult, op1=ALU.mult),
                   reads=[rs("lnA"), rs("lnB")], writes=[rs("lnC")])

            def lt_buf(cc):
                return [(sig[0], rs("sig0")), (sig[1], rs("sig1")), (lt2, rs("lt2")), (lnA, rs("lnA"))][cc]

            def ln_mul(cc):
                lt, rlt = lt_buf(cc)
                op("dve", (lambda cc, lt: lambda e: e.tensor_tensor(out=lt[:, 0:NT], in0=h_sb[:, cc, 0:NT], in1=lnB[:, 0:NT],
                                                                    op=ALU.mult))(cc, lt),
                   reads=[r_hsb[cc], rs("lnB")], writes=[rlt])
                op("pool", (lambda lt: lambda e: e.tensor_tensor(out=lt[:, 0:NT], in0=lt[:, 0:NT], in1=lnC[:, 0:NT], op=ALU.add))(lt),
                   reads=[rlt, rs("lnC")], writes=[rlt])

            def ln_silu(cc):
                lt, rlt = lt_buf(cc)
                op("act", (lambda cc, lt: lambda e: e.activation(out=mixT[:, 4 + cc, 0:NT], in_=lt[:, 0:NT], func=AF.Silu,
                                                                 scale=cpar_sb[:, cpo + 4 + cc: cpo + 5 + cc],
                                                                 bias=cpar_sb[:, cpo + 8 + cc: cpo + 9 + cc]))(cc, lt),
                   reads=[rlt, rs("cpar")], writes=[rs("mixTc")])

            attn_S(0)
            if prompt:
                for c in range(nch):
                    if c < 2:
                        conv_mm(2 * c, 0, 31)
                        conv_build(2 * c + 2) if 2 * c + 2 < 4 else None
                        attn_S(c + 1)
                        conv_mm(2 * c + 1, 0, 31)
                        conv_build(2 * c + 3) if 2 * c + 3 < 4 else None
                        if c == 1:
                            ln_stats()
                    elif c + 1 < nch:
                        attn_S(c + 1)
                    attn_PV(c)
                    if c == 2:
                        ln_a()
                    elif c == 3:
                        ln_b()
                    elif c == 4:
                        ln_c()
                        ln_mul(0)
                        ln_mul(1)
                    elif c == 5:
                        ln_mul(2)
                        ln_mul(3)
                    elif c == 6:
                        for cc_ in range(4):
                            ln_silu(cc_)
            else:
                for c in range(nch):
                    if c < 2:
                        for cc in (2 * c, 2 * c + 1):
                            conv_mm(cc, 0, 31)
                            if cc + 2 < 4:
                                conv_build(cc + 2)
                        if c == 1:
                            ln_stats()
                    if c + 1 < nch:
                        attn_S(c + 1)
                    attn_PV(c)
                    if c == 2:
                        ln_a()
                        ln_b()
                    elif c == 3:
                        ln_c()
                        ln_mul(0)
                        ln_mul(1)
                ln_mul(2)
                ln_mul(3)
                for cc_ in range(4):
                    ln_silu(cc_)

            stage(2)
            if prompt and not last:
                op("pool", lambda e: e.tensor_copy(out=KT[l][:, 0:128], in_=KT[l][:, 512:640]), reads=[rKT], writes=[rKT])
                op("pool", lambda e: e.tensor_copy(out=VT[l][:, 0:128], in_=VT[l][:, 512:640]), reads=[rVT], writes=[rVT])
                op("pool", lambda e: e.tensor_copy(out=uT[l][:, :, 0:30], in_=uT[l][:, :, 512:542]), reads=[ruT], writes=[ruT])

            stage(3)
            for half in range(2):
                slot, rslot = ws_get()

                def wo_mm(tc, kcs):
                    for kc in kcs:
                        rd_mix = [rs(f"mixT_c{2 * tc}"), rs(f"mixT_c{2 * tc + 1}")] if kc < 4 else [rs("mixTc")]
                        op("pe", (lambda tc, kc, slot: lambda e: e.matmul(
                            TK[tc][:, :], lhsT=mixT[:, kc, tc * 128:(tc + 1) * 128], rhs=slot[:, kc * 512:(kc + 1) * 512],
                            start=(kc == 0), stop=(kc == 7)))(tc, kc, slot),
                           reads=[rslot] + rd_mix, writes=[r_TK[tc]])

                if half == 0:
                    for tc in range(ntc):
                        wo_mm(tc, range(0, 4))
                for tc in range(ntc):
                    wo_mm(tc, range(4, 8) if half == 0 else range(8))
                    xs_ = xb[xi][:, tc, half * 512:(half + 1) * 512]
                    op("dve", (lambda tc, xs_: lambda e: e.tensor_tensor(out=xs_, in0=TK[tc][:, :], in1=xs_, op=ALU.add))(tc, xs_),
                       reads=[r_TK[tc], r_x[xi][tc]], writes=[r_x[xi][tc]])
                    if half == 1:
                        norm_cast(xi, tc, True)
                        if tc >= 1:
                            norm_tr(tc - 1, 1)
                ws_release()
            norm_tr(ntc - 1, 1)

            stage(4)
            if last:
                if prompt:
                    for (src, rsrc, dst) in ((kf32, rs("kf32"), nkp), (vf32, rs("vf32"), nvp)):
                        t = next_tr()
                        og = next_tr()
                        op("pe", (lambda t, src: lambda e: e.transpose(out=TR[t][:, 0:128], in_=src[:, 0:128], identity=ident_f[:]))(t, src),
                           reads=[rsrc, rs("ident_f")], writes=[r_TR[t]])
                        op("act", (lambda t, og: lambda e: e.activation(out=ostg[og][:, 0:128], in_=TR[t][:, 0:128], func=AF.Copy))(t, og),
                           reads=[r_TR[t]], writes=[r_ostg[og]])
                        dma(dst[l, b, :, :], ostg[og][:, 0:128], [r_ostg[og], OUT], [])
                    t = next_tr()
                    og = next_tr()
                    for cc in range(4):
                        op("pe", (lambda t, cc: lambda e: e.transpose(out=TR[t][0:30, cc * 128:(cc + 1) * 128], in_=uf32[:, cc, 0:30],
                                                                      identity=ident_f[:]))(t, cc),
                           reads=[rs("uf32"), rs("ident_f")], writes=[r_TR[t]])
                    op("act", (lambda t, og: lambda e: e.activation(out=ostg[og][0:30, :], in_=TR[t][0:30, :], func=AF.Copy))(t, og),
                       reads=[r_TR[t]], writes=[r_ostg[og]])
                    dma(ncp[l, b, :, :], ostg[og][0:30, :], [r_ostg[og], OUT], [])
                else:
                    for (src, rsrc, dst, cache) in ((kf32, rs("kf32"), nks, ck), (vf32, rs("vf32"), nvs, cv)):
                        t = next_tr()
                        og = next_tr()
                        for bb in range(BS):
                            op("pe", (lambda t, bb, src: lambda e: e.transpose(out=TR[t][0:64, bb * 128:(bb + 1) * 128],
                                                                               in_=src[:, bb * 64:(bb + 1) * 64], identity=ident_f[:]))(t, bb, src),
                               reads=[rsrc, rs("ident_f")], writes=[r_TR[t]])
                        op("act", (lambda t, og: lambda e: e.activation(out=ostg[og][0:64, :], in_=TR[t][0:64, :], func=AF.Copy))(t, og),
                           reads=[r_TR[t]], writes=[r_ostg[og]])
                        dma(dst[l, :, 64:128, :].rearrange("b t f -> t b f"), ostg[og][0:64, :].rearrange("p (b f) -> p b f", b=BS),
                            [r_ostg[og], OUT], [])
                        dma(dst[l, :, 0:64, :], cache[l, :, 64:128, :], [OUT], [])
                    for bb in range(BS):
                        t = next_tr()
                        og = next_tr()
                        for cc in range(4):
                            op("pe", (lambda t, cc, bb: lambda e: e.transpose(out=TR[t][0:30, cc * 128:(cc + 1) * 128],
                                                                              in_=uf32[:, cc, bb * 30:(bb + 1) * 30],
                                                                              identity=ident_f[:]))(t, cc, bb),
                               reads=[rs("uf32"), rs("ident_f")], writes=[r_TR[t]])
                        op("act", (lambda t, og: lambda e: e.activation(out=ostg[og][0:30, :], in_=TR[t][0:30, :], func=AF.Copy))(t, og),
                           reads=[r_TR[t]], writes=[r_ostg[og]])
                        dma(ncs[l, bb, :, :], ostg[og][0:30, :], [r_ostg[og], OUT], [])

            for blk8 in range(8):
                slot, rslot = ws_get()
                for mi in range(4):
                    m = blk8 * 4 + mi
                    if blk8 == 0 and mi == 0 and NT == 512:
                        pre = {0: next_acc(), 1: next_acc()}
                        for h in range(2):
                            for mi2 in (0, 1):
                                for kc in range(8):
                                    op("pe", (lambda a2, kc, mi2, slot, h: lambda e: e.matmul(
                                        ACC[a2][:, h * 256:(h + 1) * 256],
                                        lhsT=slot[:, kc * 512 + mi2 * 128: kc * 512 + (mi2 + 1) * 128],
                                        rhs=xnT[1][:, kc, h * 256:(h + 1) * 256], start=(kc == 0), stop=(kc == 7)))(pre[mi2], kc, mi2, slot, h),
                                       reads=[rslot, r_xnTp[1][h]], writes=[r_ACC[pre[mi2]]])
                    if blk8 == 0 and mi < 2 and NT == 512:
                        a = pre[mi]
                    else:
                        a = next_acc()
                        for kc in range(8):
                            op("pe", (lambda a, kc, mi, slot: lambda e: e.matmul(
                                ACC[a][:, 0:NT], lhsT=slot[:, kc * 512 + mi * 128: kc * 512 + (mi + 1) * 128],
                                rhs=xnT[1][:, kc, 0:NT], start=(kc == 0), stop=(kc == 7)))(a, kc, mi, slot),
                               reads=[rslot, r_xnT[1]], writes=[r_ACC[a]])
                    op("act", (lambda a, m: lambda e: e.activation(out=rl[m % 2][:, 0:NT], in_=ACC[a][:, 0:NT], func=AF.Relu))(a, m),
                       reads=[r_ACC[a]], writes=[rs(f"rl{m % 2}")])
                    op("dve", (lambda m: lambda e: e.tensor_tensor(out=hidT[:, m, 0:NT], in0=rl[m % 2][:, 0:NT], in1=rl[m % 2][:, 0:NT],
                                                                   op=ALU.mult))(m),
                       reads=[rs(f"rl{m % 2}")], writes=[r_hid[m // 8]])
                ws_release()

            stage(5)
            norm_defer_tail(xi, ntc)
            for half in range(2):
                for kg in range(4):
                    slot, rslot = ws_get()
                    for tc in range(ntc):
                        for kcl in range(8):
                            kc = kg * 8 + kcl
                            op("pe", (lambda tc, kc, kcl, slot: lambda e: e.matmul(
                                TK[tc][:, :], lhsT=hidT[:, kc, tc * 128:(tc + 1) * 128], rhs=slot[:, kcl * 512:(kcl + 1) * 512],
                                start=(kc == 0), stop=(kc == 31)))(tc, kc, kcl, slot),
                               reads=[rslot, r_hid[kg]], writes=[r_TK[tc]])
                    ws_release()
                for tc in range(ntc):
                    xs_ = xb[xi][:, tc, half * 512:(half + 1) * 512]
                    op("dve", (lambda tc, xs_: lambda e: e.scalar_tensor_tensor(out=xs_, in0=TK[tc][:, :], scalar=rstd2[:, tc:tc + 1],
                                                                                in1=xs_, op0=ALU.mult, op1=ALU.add))(tc, xs_),
                       reads=[r_TK[tc], r_x[xi][tc], rs("rstd2")], writes=[r_x[xi][tc]])

        def final_norm(idx):
            kind, b, ti = tiles[idx]
            xi = idx % 2
            ntc = 4 if kind == "p" else 2
            norm_stats(xi, ntc, 1)
            for tc in range(ntc):
                op("dve", (lambda tc: lambda e: e.scalar_tensor_tensor(out=xb[xi][:, tc, :], in0=xb[xi][:, tc, :],
                                                                       scalar=rstd[:, tc:tc + 1], in1=gt[:, :],
                                                                       op0=ALU.mult, op1=ALU.mult))(tc),
                   reads=[r_x[xi][tc], rs("rstd"), rs("gt")], writes=[r_x[xi][tc]])
            if kind == "p":
                dma(yp[b, ti * 512:(ti + 1) * 512, :].rearrange("(tc p) d -> p tc d", p=128), xb[xi][:, :, :], r_x[xi] + [OUT], [])
            else:
                dma(ys[:, :].rearrange("(tc p) d -> p tc d", p=128), xb[xi][:, 0:2, :], r_x[xi][0:2] + [OUT], [])

        if not DBG_NOTILES:
            x_load(0)
            stage_load(0, 0)
            stage_load(0, 1)
            for _ in range(4):
                ws_issue()
        try:
            if DBG_NOTILES:
                raise _StopBuild()
            for idx in range(len(tiles)):
                if idx >= 1 and idx + 1 < len(tiles):
                    x_load(idx + 1)
                for l in range(L):
                    layer(idx, l)
                if idx == 0 and len(tiles) > 1:
                    x_load(1)
                final_norm(idx)
        except _StopBuild:
            pass
        op("sp", None, writes=[OUT] + list(R.values()))

        with nc.Block() as block:
            T.emit(block, sems, dsems)
    return nc


def _prep_weights(norm1, w_in, conv_w, conv_b, conv_ln_g, conv_ln_b, w_out, norm2, w_up, w_down):
    wsrc = np.empty((L * NBLK, 128, 4096), np.float32)
    qcols = [np.r_[j * 64:(j + 1) * 64, (j + 4) * 64:(j + 5) * 64] for j in range(4)]
    mcols = qcols + [np.arange(512, 640), np.arange(640, 768)]
    for cc in range(4):
        mcols.append(np.arange(1280 + cc * 128, 1280 + (cc + 1) * 128))
        mcols.append(np.arange(768 + cc * 128, 768 + (cc + 1) * 128))
    orow = np.concatenate([np.r_[j * 64:(j + 1) * 64, (j + 4) * 64:(j + 5) * 64] for j in range(4)] + [np.arange(512, 1024)])
    for l in range(L):
        base = l * NBLK
        wi = w_in[l].reshape(8, 128, 1792)
        for blk in range(4):
            buf = np.zeros((128, 8, 4, 128), np.float32)
            for mi in range(4):
                m = blk * 4 + mi
                if m < 14:
                    buf[:, :, mi, :] = wi[:, :, mcols[m]].transpose(1, 0, 2)
            wsrc[base + blk] = buf.reshape(128, 4096)
        wo = w_out[l][orow].reshape(8, 128, 1024)
        for half in range(2):
            wsrc[base + 4 + half] = wo[:, :, half * 512:(half + 1) * 512].transpose(1, 0, 2).reshape(128, 4096)
        wu = w_up[l].reshape(8, 128, 32, 128)
        for blk in range(8):
            wsrc[base + 6 + blk] = wu[:, :, blk * 4:(blk + 1) * 4, :].transpose(1, 0, 2, 3).reshape(128, 4096)
        wd = w_down[l].reshape(4, 8, 128, 1024)
        for half in range(2):
            for kg in range(4):
                wsrc[base + 14 + half * 4 + kg] = wd[kg][:, :, half * 512:(half + 1) * 512].transpose(1, 0, 2).reshape(128, 4096)
    gtab = np.empty((128, L * 16), np.float32)
    cw = np.empty((128, L * 4 * 31), np.float32)
    cpar = np.empty((128, L * 12), np.float32)
    for l in range(L):
        gtab[:, l * 16:l * 16 + 8] = norm1[l].reshape(8, 128).T
        gtab[:, l * 16 + 8:l * 16 + 16] = norm2[l].reshape(8, 128).T
        cw[:, l * 124:(l + 1) * 124] = conv_w[l].T.reshape(4, 128, 31).transpose(1, 0, 2).reshape(128, 124)
        cpar[:, l * 12:l * 12 + 4] = conv_b[l].reshape(4, 128).T
        cpar[:, l * 12 + 4:l * 12 + 8] = conv_ln_g[l].reshape(4, 128).T
        cpar[:, l * 12 + 8:l * 12 + 12] = conv_ln_b[l].reshape(4, 128).T
    return wsrc, gtab, cw, cpar


_PROG_CACHE = {}


def kernel(x_prompt, x_sample, cache_k, cache_v, state_conv, norm1, w_in, attn_sink, conv_w,
           conv_b, conv_ln_g, conv_ln_b, w_out, norm2, w_up, w_down, final_norm):
    f = lambda a: np.ascontiguousarray(np.asarray(a, dtype=np.float32))
    x_prompt, x_sample, cache_k, cache_v, state_conv = map(f, (x_prompt, x_sample, cache_k, cache_v, state_conv))
    B, S, _ = x_prompt.shape
    DB, TS, _ = x_sample.shape
    BP, BS = B // NCORES, DB // NCORES
    assert TS == 64
    wsrc, gtab, cw, cpar = _prep_weights(*map(f, (norm1, w_in, conv_w, conv_b, conv_ln_g, conv_ln_b, w_out, norm2, w_up, w_down)))
    fng = f(final_norm).reshape(1, D)
    sink = f(attn_sink).reshape(1, L * 8)
    key = (S, BP, BS)
    if key not in _PROG_CACHE:
        _PROG_CACHE[key] = build_program(S, BP, BS)
    nc = _PROG_CACHE[key]
    win = cache_k.shape[2]
    in_maps = []
    for c in range(NCORES):
        in_maps.append({
            "xp": x_prompt[c * BP:(c + 1) * BP],
            "xs": x_sample[c * BS:(c + 1) * BS].reshape(BS * 64, D),
            "ck": np.ascontiguousarray(cache_k[:, c * BS:(c + 1) * BS].reshape(L, BS, win, 128)),
            "cv": np.ascontiguousarray(cache_v[:, c * BS:(c + 1) * BS].reshape(L, BS, win, 128)),
            "sc": np.ascontiguousarray(state_conv[:, c * BS:(c + 1) * BS]),
            "wsrc": wsrc, "gtab": gtab, "cw": cw, "cpar": cpar, "fng": fng, "sink": sink,
        })
    res = run_bass_kernel_spmd(nc, in_maps, core_ids=list(range(NCORES)))
    rr = res.results
    cat = lambda name, axis: np.concatenate([np.asarray(r[name]) for r in rr], axis=axis)
    y_prompt = cat("yp", 0)
    y_sample = cat("ys", 0).reshape(DB, TS, D)
    nkp = cat("nkp", 1).reshape(L, B, 128, 2, 64)
    nvp = cat("nvp", 1).reshape(L, B, 128, 2, 64)
    ncp = cat("ncp", 1)
    nks = cat("nks", 1).reshape(L, DB, 128, 2, 64)
    nvs = cat("nvs", 1).reshape(L, DB, 128, 2, 64)
    ncs = cat("ncs", 1)
    return (y_prompt, y_sample, nkp, nvp, ncp, nks, nvs, ncs)
```

```python
import numpy as np
from contextlib import ExitStack
import concourse.bass as bass
import concourse.mybir as mybir
from concourse.bass_utils import run_bass_kernel_spmd

F32 = mybir.dt.float32
BF16 = mybir.dt.bfloat16
AF = mybir.ActivationFunctionType
ALU = mybir.AluOpType

ENGS = ("pe", "act", "dve", "pool", "sp")
NCORES = 8
L = 2
D = 1024
NBLK = 22
EPS = 1e-5
TP, TD = 26, 5
STRICT_SAME_ENGINE = True


class Res:
    __slots__ = ("name", "lw", "rd")

    def __init__(self, name):
        self.name = name
        self.lw = None
        self.rd = []


class Op:
    __slots__ = ("eng", "emit", "deps", "idx", "needed", "dma", "sem", "semval", "rank", "prev_same_sem")

    def __init__(self, eng, emit, dma):
        self.eng = eng
        self.emit = emit
        self.deps = []
        self.needed = False
        self.dma = dma
        self.sem = None
        self.semval = 0
        self.rank = 0
        self.prev_same_sem = None
        self.idx = 0


class Tracker:
    def __init__(self, n_dma_sems=12):
        self.ops = {e: [] for e in ENGS}
        self.n_dma_sems = n_dma_sems
        self.dma_count = {e: 0 for e in ENGS}
        self.dma_last = {}

    def op(self, eng, emit, reads=(), writes=(), dma=False):
        o = Op(eng, emit, dma)
        deps = {}
        for r in reads:
            if r.lw is not None:
                deps[id(r.lw)] = (r.lw, "raw")
        for w in writes:
            if w.lw is not None and id(w.lw) not in deps:
                deps[id(w.lw)] = (w.lw, "waw")
            for rr in w.rd:
                if id(rr) not in deps:
                    deps[id(rr)] = (rr, "war")
        best = {}
        for d, kind in deps.values():
            if (not d.dma) and d.eng == eng and not dma:
                if eng == "pe" or (kind != "raw" and not STRICT_SAME_ENGINE):
                    continue
            if d.dma:
                o.deps.append(d)
            else:
                b = best.get(d.eng)
                if b is None or d.idx > b.idx:
                    best[d.eng] = d
        o.deps.extend(best.values())
        if dma:
            n = self.dma_count[eng]
            self.dma_count[eng] = n + 1
            slot = n % self.n_dma_sems
            o.sem = (eng, slot)
            o.semval = 16 * (n // self.n_dma_sems + 1)
            o.prev_same_sem = self.dma_last.get((eng, slot))
            self.dma_last[(eng, slot)] = o
        for r in reads:
            if dma:
                r.rd.append(o)
            else:
                r.rd = [x for x in r.rd if x.dma or x.eng != eng]
                r.rd.append(o)
        for w in writes:
            w.lw = o
            w.rd = []
        o.idx = len(self.ops[eng])
        self.ops[eng].append(o)
        return o

    def emit(self, block, sems, dma_sems):
        for e in ENGS:
            for o in self.ops[e]:
                for d in o.deps:
                    d.needed = True
        for e in ENGS:
            r = 0
            for o in self.ops[e]:
                if o.needed and not o.dma:
                    r += 1
                    o.rank = r
        trk = self

        def run(ename, eng):
            waited = {}
            for o in trk.ops[ename]:
                waits = []
                for d in o.deps:
                    if d.dma:
                        waits.append((dma_sems[d.sem], d.semval))
                    else:
                        waits.append((sems[d.eng], d.rank))
                if o.dma and o.prev_same_sem is not None:
                    p = o.prev_same_sem
                    waits.append((dma_sems[p.sem], p.semval))
                for s, v in waits:
                    k = s.num
                    if waited.get(k, 0) >= v:
                        continue
                    waited[k] = v
                    eng.wait_ge(s, v)
                if o.emit is None:
                    continue
                ins = o.emit(eng)
                if o.dma:
                    ins.then_inc(dma_sems[o.sem], 16)
                elif o.needed:
                    ins.then_inc(sems[ename], 1)

        @block.tensor
        def _(e):
            run("pe", e)

        @block.scalar
        def _(e):
            run("act", e)

        @block.vector
        def _(e):
            run("dve", e)

        @block.gpsimd
        def _(e):
            run("pool", e)

        @block.sync
        def _(e):
            run("sp", e)


class _StopBuild(Exception):
    pass


DBG_STOP = None
DBG_NPRE = None
DBG_NOTILES = False
import os as _os
DBG_SKIP = set((_os.environ.get('DBG_SKIP') or '').split(','))


def build_program(S, BP, BS):
    assert S % 512 == 0 and BS == 4
    NTS = BS * 64
    nc = bass.Bass("TRN2", target_bir_lowering=False)
    di = lambda n, s, d=F32: nc.dram_tensor(n, s, d, kind="ExternalInput")
    do = lambda n, s, d=F32: nc.dram_tensor(n, s, d, kind="ExternalOutput")
    xp = di("xp", [BP, S, D])
    xs = di("xs", [NTS, D])
    ck = di("ck", [L, BS, 128, 128])
    cv = di("cv", [L, BS, 128, 128])
    sc = di("sc", [L, BS, 30, 512])
    wsrc = di("wsrc", [L * NBLK, 128, 4096])
    gtab = di("gtab", [128, L * 2 * 8])
    cw = di("cw", [128, L * 4 * 31])
    cpar = di("cpar", [128, L * 3 * 4])
    fng = di("fng", [1, D])
    sink = di("sink", [1, L * 8])
    yp = do("yp", [BP, S, D])
    ys = do("ys", [NTS, D])
    nkp = do("nkp", [L, BP, 128, 128])
    nvp = do("nvp", [L, BP, 128, 128])
    ncp = do("ncp", [L, BP, 30, 512])
    nks = do("nks", [L, BS, 128, 128])
    nvs = do("nvs", [L, BS, 128, 128])
    ncs = do("ncs", [L, BS, 30, 512])
    wscr = nc.dram_tensor("wscr", [L * NBLK, 128, 4096], BF16)

    T = Tracker()
    es = ExitStack()
    with es:
        sb = lambda n, s, d: es.enter_context(nc.sbuf_tensor(n, s, d))
        pst = lambda n, s, d: es.enter_context(nc.psum_tensor(n, s, d))
        xb = [sb(f"xb{i}", [128, 4, 1024], F32) for i in range(2)]
        xn_tm = [sb(f"xn_tm{i}", [128, 1024], BF16) for i in range(2)]
        xnT = [sb(f"xnT{i}", [128, 8, 512], BF16) for i in range(2)]
        qT = sb("qT", [128, 4, 512], BF16)
        KT = [sb(f"KT{l}", [128, 768], BF16) for l in range(L)]
        VT = [sb(f"VT{l}", [128, 768], BF16) for l in range(L)]
        uT = [sb(f"uT{l}", [128, 4, 544], BF16) for l in range(L)]
        sig = [sb(f"sig{i}", [128, 512], F32) for i in range(2)]
        h_sb = sb("h_sb", [128, 4, 512], F32)
        lnA = sb("lnA", [128, 512], F32)
        lnB = sb("lnB", [128, 512], F32)
        lnC = sb("lnC", [128, 512], F32)
        lt2 = sb("lt2", [128, 512], F32)
        mixT = sb("mixT", [128, 8, 512], BF16)
        hidT = sb("hidT", [128, 32, 512], BF16)
        rl = [sb(f"rl{i}", [128, 512], BF16) for i in range(2)]
        PTw = [[sb(f"PTw{k}{b}", [128, 256], BF16) for b in range(2)] for k in range(2)]
        PTo = [[[sb(f"PTo{l}{k}{b}", [65, 256], BF16) for b in range(2)] for k in range(2)] for l in range(L)]
        Vw = [sb(f"Vw{i}", [128, 256], BF16) for i in range(2)]
        rcp = [sb(f"rcp{i}", [128, 256], F32) for i in range(2)]
        diag = [sb(f"diag{i}", [128, TP, 128], BF16) for i in range(2)]
        wslot = [sb(f"wslot{i}", [128, 4096], BF16) for i in range(4)]
        ident_f = sb("ident_f", [128, 128], F32)
        ident_b = sb("ident_b", [128, 128], BF16)
        ones512 = sb("ones512", [128, 128], BF16)
        ones1 = sb("ones1", [128, 64], BF16)
        gt = sb("gt", [128, 1024], F32)
        cw_sb = sb("cw_sb", [128, L * 4 * 31], F32)
        cpar_sb = sb("cpar_sb", [128, L * 3 * 4], F32)
        gtab_sb = sb("gtab_sb", [128, L * 2 * 8], F32)
        sink_sb = sb("sink_sb", [65, L * 8], F32)
        sinke_sb = sb("sinke_sb", [65, L * 8], F32)
        eps_sb = sb("eps_sb", [128, 1], F32)
        ss = sb("ss", [128, 4], F32)
        sst = sb("sst", [128, 4], F32)
        rstd = sb("rstd", [128, 4], F32)
        kf32 = sb("kf32", [128, 256], F32)
        vf32 = sb("vf32", [128, 256], F32)
        uf32 = sb("uf32", [128, 4, BS * 30], F32)
        ostg = [sb(f"ostg{i}", [128, 512], F32) for i in range(2)]
        ACC = [pst(f"ACC{i}", [128, 512], F32) for i in range(2)]
        TR = [pst(f"TR{i}", [128, 512], F32) for i in range(2)]
        TK = [pst(f"TK{i}", [128, 512], F32) for i in range(4)]
        TRb = [t.bitcast(BF16) for t in TR]
        R = {}

        def rs(name):
            if name not in R:
                R[name] = Res(name)
            return R[name]

        r_x = [[rs(f"x{i}_{tc}") for tc in range(4)] for i in range(2)]
        r_xn = [rs("xn0"), rs("xn1")]
        r_xnT = [rs("xnT0"), rs("xnT1")]
        r_xnTp = [[rs(f"xnT{w}p{h}") for h in range(2)] for w in range(2)]
        r_hid = [rs(f"hid{g}") for g in range(4)]
        r_ACC = [rs("ACC0"), rs("ACC1")]
        r_TR = [rs("TR0"), rs("TR1")]
        r_TK = [rs(f"TK{i}") for i in range(4)]
        r_slot = [rs(f"slot{i}") for i in range(4)]
        r_scr = [rs(f"scr{i}") for i in range(L * NBLK)]
        r_hsb = [rs(f"hsb{c}") for c in range(4)]
        r_ostg = [rs("ostg0"), rs("ostg1")]
        OUT = rs("OUT")

        sems = {e: es.enter_context(nc.semaphore("s_" + e)) for e in ENGS}
        dsems = {("sp", i): es.enter_context(nc.semaphore(f"d_sp{i}")) for i in range(T.n_dma_sems)}

        op = T.op

        def dma(out, in_, reads, writes):
            return op("sp", lambda e: e.dma_start(out=out, in_=in_), reads=reads, writes=writes, dma=True)

        op("pool", lambda e: e.memset(ident_f[:], 0.0), writes=[rs("ident_f")])
        op("pool", lambda e: e.affine_select(out=ident_f[:], in_=ident_f[:], pattern=[[-1, 128]],
                                             compare_op=ALU.not_equal, fill=1.0, base=0, channel_multiplier=1),
           reads=[rs("ident_f")], writes=[rs("ident_f")])
        op("pool", lambda e: e.tensor_copy(out=ident_b[:], in_=ident_f[:]), reads=[rs("ident_f")], writes=[rs("ident_b")])
        op("pool", lambda e: e.memset(ones512[:], 1.0 / 512.0), writes=[rs("ones512")])
        op("pool", lambda e: e.memset(ones1[:], 1.0), writes=[rs("ones1")])
        op("pool", lambda e: e.memset(eps_sb[:], EPS), writes=[rs("eps")])
        dma(gt[:], fng[0:1, :].partition_broadcast(128), [], [rs("gt")])
        dma(cw_sb[:], cw[:, :], [], [rs("cw")])
        dma(cpar_sb[:], cpar[:, :], [], [rs("cpar")])
        dma(gtab_sb[:], gtab[:, :], [], [rs("gtab")])
        dma(sink_sb[64:65, :], sink[0:1, :], [], [rs("sink")])
        op("act", lambda e: e.activation(out=sinke_sb[64:65, :], in_=sink_sb[64:65, :], func=AF.Exp),
           reads=[rs("sink")], writes=[rs("sinke")])
        for l in range(L):
            for k in range(2):
                for b in range(2):
                    src = sinke_sb[64:65, l * 8 + k * 4: l * 8 + k * 4 + 4].unsqueeze(2).broadcast_to([1, 4, 64])
                    dst = PTo[l][k][b][64:65, :].rearrange("p (g q) -> p g q", g=4)
                    op("pool", (lambda dst, src: lambda e: e.tensor_copy(out=dst, in_=src))(dst, src),
                       reads=[rs("sinke")], writes=[rs(f"PTo{l}{k}{b}")])

        class WS:
            n = 0
            issued = 0
            total = 0

        NW = L * NBLK
        cvt_rr = [0]
        stg = xb[1][:].rearrange("p a b -> p (a b)")
        r_stg = [rs("stgA"), rs("stgB")]

        def stage_load(blk, h):
            dma(stg[:, h * 2048:(h + 1) * 2048], wsrc[blk, :, h * 2048:(h + 1) * 2048], [], [r_stg[h]])

        def ws_issue():
            if WS.issued >= WS.total:
                return
            n = WS.issued
            WS.issued += 1
            blk = n % NW
            s_ = n % 4
            if n >= NW:
                dma(wslot[s_][:], wscr[blk, :, :], [r_scr[blk]], [r_slot[s_]])
                return
            l_, bl = divmod(blk, NBLK)
            ws_ = wslot[s_]
            scaled = None
            if bl < 4:
                scaled = l_ * 16 + 0
            elif 6 <= bl < 14:
                scaled = l_ * 16 + 8
            for h in range(2):
                for kc in range(4 * h, 4 * h + 4):
                    o_ = ws_[:, kc * 512:(kc + 1) * 512]
                    i_ = stg[:, kc * 512:(kc + 1) * 512]
                    eng = ("act", "dve")[cvt_rr[0] % 2]
                    cvt_rr[0] += 1
                    rd = [r_stg[h]]
                    if scaled is not None:
                        sc_ap = gtab_sb[:, scaled + kc: scaled + kc + 1]
                        rd.append(rs("gtab"))
                        if eng == "act":
                            f = (lambda o_, i_, s2: lambda e: e.activation(out=o_, in_=i_, func=AF.Copy, scale=s2))(o_, i_, sc_ap)
                        else:
                            f = (lambda o_, i_, s2: lambda e: e.tensor_scalar_mul(out=o_, in0=i_, scalar1=s2))(o_, i_, sc_ap)
                    else:
                        if eng == "act":
                            f = (lambda o_, i_: lambda e: e.activation(out=o_, in_=i_, func=AF.Copy))(o_, i_)
                        else:
                            f = (lambda o_, i_: lambda e: e.tensor_copy(out=o_, in_=i_))(o_, i_)
                    op(eng, f, reads=rd, writes=[r_slot[s_]])
                if blk + 1 < NW:
                    stage_load(blk + 1, h)
            dma(wscr[blk, :, :], ws_[:], [r_slot[s_]], [r_scr[blk]])

        def ws_get():
            s = WS.n % 4
            return wslot[s], r_slot[s]

        def ws_release():
            WS.n += 1
            ws_issue()

        tiles = []
        for b in range(BP):
            for ti in range(S // 512):
                tiles.append(("p", b, ti))
        tiles.append(("s", 0, 0))
        WS.total = len(tiles) * L * NBLK

        def x_load(idx):
            kind, b, ti = tiles[idx]
            xi = idx % 2
            if kind == "p":
                dma(xb[xi][:, :, :], xp[b, ti * 512:(ti + 1) * 512, :].rearrange("(tc p) d -> p tc d", p=128), [],
                    r_x[xi] + (r_stg if idx == 1 else []))
            else:
                dma(xb[xi][:, 0:2, :], xs[:, :].rearrange("(tc p) d -> p tc d", p=128), [], r_x[xi][0:2] + (r_stg if idx == 1 else []))

        acc_i = [0]
        tr_i = [0]

        def next_acc():
            a = acc_i[0] % 2
            acc_i[0] += 1
            return a

        def next_tr():
            a = tr_i[0] % 2
            tr_i[0] += 1
            return a

        rstd2 = sb("rstd2", [128, 4], F32)

        def norm_stats(xi, ntc, jsel):
            for tc in range(ntc):
                jk = xnT[jsel][:, 2 * tc:2 * tc + 2, :].rearrange("p a b -> p (a b)")
                op("act", (lambda tc, jk: lambda e: e.activation(out=jk, in_=xb[xi][:, tc, :], func=AF.Square,
                                                                 accum_out=ss[:, tc:tc + 1]))(tc, jk),
                   reads=[r_x[xi][tc]], writes=[r_xnT[jsel], rs(f"ss{tc}")])
            op("act", lambda e: e.activation(out=sst[:, 0:ntc], in_=ss[:, 0:ntc], func=AF.Ln, scale=1.0 / D, bias=eps_sb[:, 0:1]),
               reads=[rs(f"ss{t_}") for t_ in range(ntc)] + [rs("eps")], writes=[rs("sst")])
            op("act", lambda e: e.activation(out=rstd[:, 0:ntc], in_=sst[:, 0:ntc], func=AF.Exp, scale=-0.5),
               reads=[rs("sst")], writes=[rs("rstd")])

        def norm_cast(xi, tc, defer):
            xn = xn_tm[tc % 2]
            if defer:
                op("dve", (lambda tc, xn: lambda e: e.tensor_copy(out=xn[:], in_=xb[xi][:, tc, :]))(tc, xn),
                   reads=[r_x[xi][tc]], writes=[r_xn[tc % 2]])
            else:
                op("dve", (lambda tc, xn: lambda e: e.tensor_scalar_mul(out=xn[:], in0=xb[xi][:, tc, :],
                                                                        scalar1=rstd[:, tc:tc + 1]))(tc, xn),
                   reads=[r_x[xi][tc], rs("rstd")], writes=[r_xn[tc % 2]])

        def norm_tr(tc, which):
            xn = xn_tm[tc % 2]
            t = next_tr()
            for kc in range(8):
                op("pe", (lambda kc, xn, t: lambda e: e.transpose(out=TRb[t][:, kc * 128:(kc + 1) * 128],
                                                                  in_=xn[:, kc * 128:(kc + 1) * 128],
                                                                  identity=ident_b[:]))(kc, xn, t),
                   reads=[r_xn[tc % 2], rs("ident_b")], writes=[r_TR[t]])
            op("act", (lambda tc, t: lambda e: e.activation(
                out=xnT[which][:, :, tc * 128:(tc + 1) * 128],
                in_=TRb[t][:, :].rearrange("p (k c) -> p k c", k=8), func=AF.Copy))(tc, t),
               reads=[r_TR[t]], writes=[r_xnT[which], r_xnTp[which][tc // 2]])

        def norm_T(xi, ntc, which):
            norm_stats(xi, ntc, which)
            for tc in range(ntc):
                norm_cast(xi, tc, False)
                norm_tr(tc, which)

        def norm_defer_tail(xi, ntc):
            norm_stats(xi, ntc, 1)
            op("dve", lambda e: e.tensor_tensor(out=rstd2[:, 0:ntc], in0=rstd[:, 0:ntc], in1=rstd[:, 0:ntc], op=ALU.mult),
               reads=[rs("rstd")], writes=[rs("rstd2")])

        def layer(idx, l):
            kind, b, ti = tiles[idx]
            xi = idx % 2
            prompt = kind == "p"
            ntc = 4 if prompt else 2
            NT = ntc * 128
            nch = NT // 64
            first = prompt and ti == 0
            last = (not prompt) or ti == S // 512 - 1
            kt0 = 128 if prompt else 512

            def stage(n):
                if DBG_STOP is not None and (idx, l, n) == tuple(DBG_STOP):
                    raise _StopBuild()

            rKT, rVT, ruT = rs(f"KT{l}"), rs(f"VT{l}"), rs(f"uT{l}")
            cpo = l * 12

            if first:
                op("pool", lambda e: e.memset(KT[l][:, 0:128], 0.0), writes=[rKT])
                op("pool", lambda e: e.memset(VT[l][:, 0:128], 0.0), writes=[rVT])
                op("pool", lambda e: e.memset(uT[l][:, :, 0:30], 0.0), writes=[ruT])
            if not prompt:
                stK = hidT[:, 0:4, :].bitcast(F32).rearrange("p a b -> p (a b)")
                stV = hidT[:, 4:8, :].bitcast(F32).rearrange("p a b -> p (a b)")
                stC = hidT[:, 8:24, :].bitcast(F32).rearrange("p a b -> p (a b)")
                if l == 1:
                    stK = hidT[:, 24:28, :].bitcast(F32).rearrange("p a b -> p (a b)")
                    stV = hidT[:, 28:32, :].bitcast(F32).rearrange("p a b -> p (a b)")
                dma(stK[:, 0:512].rearrange("p (b f) -> p b f", b=BS), ck[l].rearrange("b t f -> t b f"), [], r_hid)
                dma(stV[:, 0:512].rearrange("p (b f) -> p b f", b=BS), cv[l].rearrange("b t f -> t b f"), [], r_hid)
                dma(stC[0:30, 0:2048].rearrange("p (b f) -> p b f", b=BS), sc[l].rearrange("b t f -> t b f"), [], r_hid)
                t = next_tr()
                for bb in range(BS):
                    op("pe", (lambda bb, t: lambda e: e.transpose(out=TR[t][:, bb * 128:(bb + 1) * 128],
                                                                  in_=stK[:, bb * 128:(bb + 1) * 128],
                                                                  identity=ident_f[:]))(bb, t),
                       reads=r_hid + [rs("ident_f")], writes=[r_TR[t]])
                op("act", (lambda t: lambda e: e.activation(out=KT[l][:, 0:512], in_=TR[t][:, :], func=AF.Copy))(t),
                   reads=[r_TR[t]], writes=[rKT])
                op("dve", lambda e: e.tensor_copy(out=VT[l][:, 0:512], in_=stV[:, 0:512]), reads=r_hid, writes=[rVT])
                t = next_tr()
                for cc in range(4):
                    for bb in range(BS):
                        c0 = (cc * BS + bb) * 30
                        op("pe", (lambda cc, bb, c0, t: lambda e: e.transpose(
                            out=TR[t][:, c0:c0 + 30], in_=stC[0:30, bb * 512 + cc * 128: bb * 512 + (cc + 1) * 128],
                            identity=ident_f[0:30, 0:30]))(cc, bb, c0, t),
                           reads=r_hid + [rs("ident_f")], writes=[r_TR[t]])
                op("act", (lambda t: lambda e: e.activation(
                    out=uT[l][:, :, 0:BS * 94].rearrange("p c (b t) -> p c b t", b=BS)[:, :, :, 0:30],
                    in_=TR[t][:, 0:480].rearrange("p (c b t) -> p c b t", c=4, b=BS), func=AF.Copy))(t),
                   reads=[r_TR[t]], writes=[ruT])

            def conv_build(cc):
                dg = diag[cc % 2]
                for j0 in range(0, TP, 8):
                    j1 = min(TP, j0 + 8)
                    wv = cw_sb[:, (l * 4 + cc) * 31 + j0:(l * 4 + cc) * 31 + j1]
                    op("pool", (lambda dg, wv, j0, j1: lambda e: e.tensor_tensor(
                        out=dg[:, j0:j1, :], in0=ident_b[:].unsqueeze(1).broadcast_to([128, j1 - j0, 128]),
                        in1=wv.unsqueeze(2).broadcast_to([128, j1 - j0, 128]), op=ALU.mult))(dg, wv, j0, j1),
                       reads=[rs("ident_b"), rs("cw")], writes=[rs(f"diag{cc % 2}")])

            stage(-1)
            norm_T(xi, ntc, 0)
            stage(0)
            conv_build(0)
            conv_build(1)

            mlist = ["q0", "q1", "q2", "q3", "k", "v", "g0", "a0", "g1", "a1", "g2", "a2", "g3", "a3"]
            for blk4 in range(4):
                slot, rslot = ws_get()
                for mi in range(4):
                    m = blk4 * 4 + mi
                    if m >= 14:
                        break
                    name = mlist[m]
                    if blk4 == 0 and mi == 0 and NT == 512:
                        pre = {0: next_acc(), 1: next_acc()}
                        for h in range(2):
                            for mi2 in (0, 1):
                                for kc in range(8):
                                    op("pe", (lambda a2, kc, mi2, slot, h: lambda e: e.matmul(
                                        ACC[a2][:, h * 256:(h + 1) * 256],
                                        lhsT=slot[:, kc * 512 + mi2 * 128: kc * 512 + (mi2 + 1) * 128],
                                        rhs=xnT[0][:, kc, h * 256:(h + 1) * 256], start=(kc == 0), stop=(kc == 7)))(pre[mi2], kc, mi2, slot, h),
                                       reads=[rslot, r_xnTp[0][h]], writes=[r_ACC[pre[mi2]]])
                    if blk4 == 0 and mi < 2 and NT == 512:
                        a = pre[mi]
                    else:
                        a = next_acc()
                        for kc in range(8):
                            op("pe", (lambda a, kc, mi, slot: lambda e: e.matmul(
                                ACC[a][:, 0:NT], lhsT=slot[:, kc * 512 + mi * 128: kc * 512 + (mi + 1) * 128],
                                rhs=xnT[0][:, kc, 0:NT], start=(kc == 0), stop=(kc == 7)))(a, kc, mi, slot),
                               reads=[rslot, r_xnT[0]], writes=[r_ACC[a]])
                    if name[0] in DBG_SKIP:
                        continue
                    if name[0] == "q":
                        j = int(name[1])
                        op("act", (lambda a, j: lambda e: e.activation(out=qT[:, j, 0:NT], in_=ACC[a][:, 0:NT],
                                                                       func=AF.Copy, scale=0.125))(a, j),
                           reads=[r_ACC[a]], writes=[rs("qT")])
                    elif name == "k":
                        if last:
                            c0 = NT - 128 if prompt else 0
                            w_ = 128 if prompt else 256
                            op("act", (lambda a, c0, w_: lambda e: e.activation(out=kf32[:, 0:w_], in_=ACC[a][:, c0:c0 + w_], func=AF.Copy))(a, c0, w_),
                               reads=[r_ACC[a]], writes=[rs("kf32")])
                        op("act", (lambda a: lambda e: e.activation(out=KT[l][:, kt0:kt0 + NT], in_=ACC[a][:, 0:NT],
                                                                    func=AF.Copy))(a),
                           reads=[r_ACC[a]], writes=[rKT])
                    elif name == "v":
                        if last:
                            c0 = NT - 128 if prompt else 0
                            w_ = 128 if prompt else 256
                            op("act", (lambda a, c0, w_: lambda e: e.activation(out=vf32[:, 0:w_], in_=ACC[a][:, c0:c0 + w_], func=AF.Copy))(a, c0, w_),
                               reads=[r_ACC[a]], writes=[rs("vf32")])
                        op("act", (lambda a: lambda e: e.activation(out=VT[l][:, kt0:kt0 + NT], in_=ACC[a][:, 0:NT], func=AF.Copy))(a),
                           reads=[r_ACC[a]], writes=[rVT])
                    elif name[0] == "g":
                        cc = int(name[1])
                        op("act", (lambda a, cc: lambda e: e.activation(out=sig[cc % 2][:, 0:NT], in_=ACC[a][:, 0:NT],
                                                                        func=AF.Sigmoid))(a, cc),
                           reads=[r_ACC[a]], writes=[rs(f"sig{cc % 2}")])
                    else:
                        cc = int(name[1])
                        if prompt:
                            o_ = uT[l][:, cc, 30:30 + NT]
                            i0 = ACC[a][:, 0:NT]
                            i1 = sig[cc % 2][:, 0:NT]
                        else:
                            o_ = uT[l][:, cc, 0:BS * 94].rearrange("p (b t) -> p b t", b=BS)[:, :, 30:94]
                            i0 = ACC[a][:, 0:NT].rearrange("p (b t) -> p b t", b=BS)
                            i1 = sig[cc % 2][:, 0:NT].rearrange("p (b t) -> p b t", b=BS)
                        op("dve", (lambda o_, i0, i1: lambda e: e.tensor_tensor(out=o_, in0=i0, in1=i1, op=ALU.mult))(o_, i0, i1),
                           reads=[r_ACC[a], rs(f"sig{cc % 2}")], writes=[ruT])
                        if last:
                            if prompt:
                                o2 = uf32[:, cc, 0:30]
                                j0 = ACC[a][:, NT - 30:NT]
                                j1 = sig[cc % 2][:, NT - 30:NT]
                            else:
                                o2 = uf32[:, cc, :].rearrange("p (b t) -> p b t", b=BS)
                                j0 = ACC[a][:, 0:NT].rearrange("p (b t) -> p b t", b=BS)[:, :, 34:64]
                                j1 = sig[cc % 2][:, 0:NT].rearrange("p (b t) -> p b t", b=BS)[:, :, 34:64]
                            op("dve", (lambda o2, j0, j1: lambda e: e.tensor_tensor(out=o2, in0=j0, in1=j1, op=ALU.mult))(o2, j0, j1),
                               reads=[r_ACC[a], rs(f"sig{cc % 2}")], writes=[rs("uf32")])
                ws_release()

            def u_tap(cc, j):
                if prompt:
                    return uT[l][:, cc, j:j + NT]
                return uT[l][:, cc, 0:BS * 94].rearrange("p (b t) -> p b t", b=BS)[:, :, j:j + 64]

            def conv_side_taps(cc):
                bcol = cpar_sb[:, cpo + cc: cpo + cc + 1]
                if prompt:
                    o_ = h_sb[:, cc, 0:NT]
                else:
                    o_ = h_sb[:, cc, 0:NT].rearrange("p (b t) -> p b t", b=BS)
                for j in range(TP, 31):
                    u_ = u_tap(cc, j)
                    w_ = cw_sb[:, (l * 4 + cc) * 31 + j:(l * 4 + cc) * 31 + j + 1]
                    if j == TP:
                        f = (lambda u_, w_: lambda e: e.tensor_scalar(out=o_, in0=u_, scalar1=w_, scalar2=bcol,
                                                                      op0=ALU.mult, op1=ALU.add))(u_, w_)
                    else:
                        f = (lambda u_, w_: lambda e: e.scalar_tensor_tensor(out=o_, in0=u_, scalar=w_, in1=o_,
                                                                             op0=ALU.mult, op1=ALU.add))(u_, w_)
                    op("dve", f, reads=[ruT, rs("cw"), rs("cpar"), r_hsb[cc]], writes=[r_hsb[cc]])

            stage(1)

            def attn_S(c):
                if "A" in DBG_SKIP:
                    return
                C = ti * 8 + c if prompt else 2
                bsel = c % 2
                if prompt:
                    KTwin = KT[l][:, c * 64: c * 64 + 128]
                    KTown = KT[l][:, 128 + c * 64: 128 + c * 64 + 64]
                else:
                    KTwin = KT[l][:, c * 128:(c + 1) * 128]
                    KTown = KT[l][:, 512 + c * 64: 512 + (c + 1) * 64]
                vw = Vw[bsel]
                rvw = rs(f"Vw{bsel}")
                t = next_tr()
                if prompt:
                    if C >= 1:
                        op("pe", (lambda t, c: lambda e: e.transpose(out=TRb[t][:, 0:128], in_=VT[l][:, c * 64: c * 64 + 128],
                                                                     identity=ident_b[:]))(t, c),
                           reads=[rVT, rs("ident_b")], writes=[r_TR[t]])
                    op("pe", (lambda t, c: lambda e: e.transpose(out=TRb[t][0:64, 128:256],
                                                                 in_=VT[l][:, 128 + c * 64: 128 + (c + 1) * 64],
                                                                 identity=ident_b[:]))(t, c),
                       reads=[rVT, rs("ident_b")], writes=[r_TR[t]])
                    if C >= 1:
                        op("act", (lambda t, vw: lambda e: e.activation(out=vw[:, 0:128], in_=TRb[t][:, 0:128], func=AF.Copy))(t, vw),
                           reads=[r_TR[t]], writes=[rvw])
                    op("act", (lambda t, vw: lambda e: e.activation(out=vw[0:64, 128:256], in_=TRb[t][0:64, 128:256], func=AF.Copy))(t, vw),
                       reads=[r_TR[t]], writes=[rvw])
                else:
                    op("pe", (lambda t, c: lambda e: e.transpose(out=TRb[t][0:64, 128:256],
                                                                 in_=VT[l][:, 512 + c * 64: 512 + (c + 1) * 64],
                                                                 identity=ident_b[:]))(t, c),
                       reads=[rVT, rs("ident_b")], writes=[r_TR[t]])
                    op("act", (lambda t, vw: lambda e: e.activation(out=vw[0:64, 128:256], in_=TRb[t][0:64, 128:256], func=AF.Copy))(t, vw),
                       reads=[r_TR[t]], writes=[rvw])
                for k in range(2):
                    rq = qT[64 * k:64 * k + 64, :, c * 64:(c + 1) * 64]
                    if C >= 1:
                        op("pe", (lambda k, rq, KTwin: lambda e: e.matmul(TK[k][:, 0:256], lhsT=KTwin[64 * k:64 * k + 64, :],
                                                                          rhs=rq, start=True, stop=True))(k, rq, KTwin),
                           reads=[rKT, rs("qT")], writes=[r_TK[k]])
                    op("pe", (lambda k, rq, KTown: lambda e: e.matmul(TK[k][0:64, 256:512], lhsT=KTown[64 * k:64 * k + 64, :],
                                                                      rhs=rq, start=True, stop=True))(k, rq, KTown),
                       reads=[rKT, rs("qT")], writes=[r_TK[k]])
                    if C >= 1:
                        op("act", (lambda k: lambda e: e.activation(out=PTw[k][bsel][:, :], in_=TK[k][:, 0:256], func=AF.Exp))(k),
                           reads=[r_TK[k]], writes=[rs(f"PTw{k}{bsel}")])
                    if C == 1:
                        op("pool", (lambda k: lambda e: e.memset(PTw[k][bsel][0:64, :], 0.0))(k),
                           reads=[rs(f"PTw{k}{bsel}")], writes=[rs(f"PTw{k}{bsel}")])
                    op("act", (lambda k: lambda e: e.activation(out=PTo[l][k][bsel][0:64, :], in_=TK[k][0:64, 256:512], func=AF.Exp))(k),
                       reads=[r_TK[k]], writes=[rs(f"PTo{l}{k}{bsel}")])

            def attn_PV(c):
                if "A" in DBG_SKIP or "P" in DBG_SKIP:
                    return
                C = ti * 8 + c if prompt else 2
                bsel = c % 2
                vw = Vw[bsel]
                rvw = rs(f"Vw{bsel}")
                od = 2 + bsel
                for k in range(2):
                    ptw, pto = PTw[k][bsel], PTo[l][k][bsel]
                    rptw, rpto = rs(f"PTw{k}{bsel}"), rs(f"PTo{l}{k}{bsel}")
                    orow = slice(64 * k, 64 * k + 64)
                    if C >= 1:
                        if prompt:
                            lw = vw[:, 64 * k:64 * k + 64]
                            rdv = [rvw]
                        else:
                            lw = VT[l][:, c * 128 + 64 * k: c * 128 + 64 * k + 64]
                            rdv = [rVT]
                        op("pe", (lambda lw, ptw, orow: lambda e: e.matmul(TK[od][orow, 0:256], lhsT=lw, rhs=ptw[:, :],
                                                                           start=True, stop=False))(lw, ptw, orow),
                           reads=rdv + [rptw], writes=[r_TK[od]])
                    op("pe", (lambda pto, orow, k: lambda e: e.matmul(TK[od][orow, 0:256], lhsT=vw[0:64, 128 + 64 * k:128 + 64 * k + 64],
                                                                      rhs=pto[0:64, :], start=(C == 0), stop=True))(pto, orow, k),
                       reads=[rvw, rpto], writes=[r_TK[od]])
                    if C >= 1:
                        op("pe", (lambda ptw, orow: lambda e: e.matmul(TK[od][orow, 256:512], lhsT=ones1[:, :], rhs=ptw[:, :],
                                                                       start=True, stop=False))(ptw, orow),
                           reads=[rs("ones1"), rptw], writes=[r_TK[od]])
                    op("pe", (lambda pto, orow: lambda e: e.matmul(TK[od][orow, 256:512], lhsT=ones1[0:65, :], rhs=pto[0:65, :],
                                                                   start=(C == 0), stop=True))(pto, orow),
                       reads=[rs("ones1"), rpto], writes=[r_TK[od]])
                if "N" in DBG_SKIP:
                    return
                op("dve", lambda e: e.reciprocal(out=rcp[bsel][:, :], in_=TK[od][:, 256:512]),
                   reads=[r_TK[od]], writes=[rs(f"rcp{bsel}")])
                op("dve", lambda e: e.tensor_tensor(out=mixT[:, 0:4, c * 64:(c + 1) * 64],
                                                    in0=TK[od][:, 0:256].rearrange("p (g q) -> p g q", g=4),
                                                    in1=rcp[bsel][:, :].rearrange("p (g q) -> p g q", g=4), op=ALU.mult),
                   reads=[r_TK[od], rs(f"rcp{bsel}")], writes=[rs(f"mixT_c{c}")])

            conv_state = {}

            def conv_mm(cc, j0, j1):
                if "C" in DBG_SKIP:
                    return
                dg = diag[cc % 2]
                conv_side_taps(cc)
                a = next_acc()
                for j in range(TP):
                    op("pe", (lambda a, j, rhs, dg: lambda e: e.matmul(ACC[a][:, 0:NT], lhsT=dg[:, j, :], rhs=rhs,
                                                                       start=(j == 0), stop=(j == TP - 1)))(a, j, u_tap(cc, j), dg),
                       reads=[rs(f"diag{cc % 2}"), ruT], writes=[r_ACC[a]])
                op("dve", (lambda a, cc: lambda e: e.tensor_tensor(out=h_sb[:, cc, 0:NT], in0=ACC[a][:, 0:NT],
                                                                   in1=h_sb[:, cc, 0:NT], op=ALU.add))(a, cc),
                   reads=[r_ACC[a], r_hsb[cc]], writes=[r_hsb[cc]])
                op("act", (lambda cc: lambda e: e.activation(out=xnT[0][:, 4 + cc, 0:NT], in_=h_sb[:, cc, 0:NT],
                                                             func=AF.Square))(cc),
                   reads=[r_hsb[cc]], writes=[rs("hsq")])
                op("act", (lambda cc: lambda e: e.activation(out=xnT[0][:, cc, 0:NT], in_=h_sb[:, cc, 0:NT],
                                                             func=AF.Copy))(cc),
                   reads=[r_hsb[cc]], writes=[rs("hb")])

            R["hb"] = r_xnT[0]
            R["hsq"] = r_xnT[0]

            ln_st = {}

            def ln_stats():
                a_mean = next_acc()
                for cc in range(4):
                    op("pe", (lambda cc: lambda e: e.matmul(ACC[a_mean][:, 0:NT], lhsT=ones512[:, :], rhs=xnT[0][:, cc, 0:NT],
                                                            start=(cc == 0), stop=(cc == 3)))(cc),
                       reads=[rs("ones512"), r_xnT[0]], writes=[r_ACC[a_mean]])
                a_ex2 = next_acc()
                for cc in range(4):
                    op("pe", (lambda cc: lambda e: e.matmul(ACC[a_ex2][:, 0:NT], lhsT=ones512[:, :], rhs=xnT[0][:, 4 + cc, 0:NT],
                                                            start=(cc == 0), stop=(cc == 3)))(cc),
                       reads=[rs("ones512"), r_xnT[0]], writes=[r_ACC[a_ex2]])
                ln_st["m"], ln_st["e"] = a_mean, a_ex2

            def ln_a():
                a_mean, a_ex2 = ln_st["m"], ln_st["e"]
                op("act", lambda e: e.activation(out=lnA[:, 0:NT], in_=ACC[a_mean][:, 0:NT], func=AF.Copy),
                   reads=[r_ACC[a_mean]], writes=[rs("lnA")])
                op("dve", lambda e: e.tensor_tensor(out=lnB[:, 0:NT], in0=lnA[:, 0:NT], in1=lnA[:, 0:NT], op=ALU.mult),
                   reads=[rs("lnA")], writes=[rs("lnB")])
                op("dve", lambda e: e.tensor_tensor(out=lnB[:, 0:NT], in0=ACC[a_ex2][:, 0:NT], in1=lnB[:, 0:NT], op=ALU.subtract),
                   reads=[r_ACC[a_ex2], rs("lnB")], writes=[rs("lnB")])
                op("dve", lambda e: e.tensor_scalar_add(out=lnB[:, 0:NT], in0=lnB[:, 0:NT], scalar1=EPS),
                   reads=[rs("lnB")], writes=[rs("lnB")])

            def ln_b():
                op("act", lambda e: e.activation(out=lnB[:, 0:NT], in_=lnB[:, 0:NT], func=AF.Ln),
                   reads=[rs("lnB")], writes=[rs("lnB")])
                op("act", lambda e: e.activation(out=lnB[:, 0:NT], in_=lnB[:, 0:NT], func=AF.Exp, scale=-0.5),
                   reads=[rs("lnB")], writes=[rs("lnB")])

            def ln_c():
                op("dve", lambda e: e.scalar_tensor_tensor(out=lnC[:, 0:NT], in0=lnA[:, 0:NT], scalar=-1.0, in1=lnB[:, 0:NT],
                                                           op0=ALU.mult, op1=ALU.mult),
                   reads=[rs("lnA"), rs("lnB")], writes=[rs("lnC")])

            def lt_buf(cc):
                return [(sig[0], rs("sig0")), (sig[1], rs("sig1")), (lt2, rs("lt2")), (lnA, rs("lnA"))][cc]

            def ln_mul(cc):
                lt, rlt = lt_buf(cc)
                op("dve", (lambda cc, lt: lambda e: e.tensor_tensor(out=lt[:, 0:NT], in0=h_sb[:, cc, 0:NT], in1=lnB[:, 0:NT],
                                                                    op=ALU.mult))(cc, lt),
                   reads=[r_hsb[cc], rs("lnB")], writes=[rlt])
                op("pool", (lambda lt: lambda e: e.tensor_tensor(out=lt[:, 0:NT], in0=lt[:, 0:NT], in1=lnC[:, 0:NT], op=ALU.add))(lt),
                   reads=[rlt, rs("lnC")], writes=[rlt])

            def ln_silu(cc):
                lt, rlt = lt_buf(cc)
                op("act", (lambda cc, lt: lambda e: e.activation(out=mixT[:, 4 + cc, 0:NT], in_=lt[:, 0:NT], func=AF.Silu,
                                                                 scale=cpar_sb[:, cpo + 4 + cc: cpo + 5 + cc],
                                                                 bias=cpar_sb[:, cpo + 8 + cc: cpo + 9 + cc]))(cc, lt),
                   reads=[rlt, rs("cpar")], writes=[rs("mixTc")])

            attn_S(0)
            if prompt:
                for c in range(nch):
                    if c < 2:
                        conv_mm(2 * c, 0, 31)
                        conv_build(2 * c + 2) if 2 * c + 2 < 4 else None
                        attn_S(c + 1)
                        conv_mm(2 * c + 1, 0, 31)
                        conv_build(2 * c + 3) if 2 * c + 3 < 4 else None
                    elif c + 1 < nch:
                        attn_S(c + 1)
                    attn_PV(c)
                    if c == 1:
                        ln_stats()
                    if c == 2:
                        ln_a()
                    elif c == 3:
                        ln_b()
                    elif c == 4:
                        ln_c()
                        ln_mul(0)
                        ln_mul(1)
                    elif c == 5:
                        ln_mul(2)
                        ln_mul(3)
                    elif c == 6:
                        for cc_ in range(4):
                            ln_silu(cc_)
            else:
                for c in range(nch):
                    if c < 2:
                        for cc in (2 * c, 2 * c + 1):
                            conv_mm(cc, 0, 31)
                            if cc + 2 < 4:
                                conv_build(cc + 2)
                        if c == 1:
                            ln_stats()
                    if c + 1 < nch:
                        attn_S(c + 1)
                    attn_PV(c)
                    if c == 2:
                        ln_a()
                        ln_b()
                    elif c == 3:
                        ln_c()
                        ln_mul(0)
                        ln_mul(1)
                ln_mul(2)
                ln_mul(3)
                for cc_ in range(4):
                    ln_silu(cc_)

            stage(2)
            if prompt and not last:
                op("pool", lambda e: e.tensor_copy(out=KT[l][:, 0:128], in_=KT[l][:, 512:640]), reads=[rKT], writes=[rKT])
                op("pool", lambda e: e.tensor_copy(out=VT[l][:, 0:128], in_=VT[l][:, 512:640]), reads=[rVT], writes=[rVT])
                op("pool", lambda e: e.tensor_copy(out=uT[l][:, :, 0:30], in_=uT[l][:, :, 512:542]), reads=[ruT], writes=[ruT])

            stage(3)
            for half in range(2):
                slot, rslot = ws_get()

                def wo_mm(tc, kcs):
                    for kc in kcs:
                        rd_mix = [rs(f"mixT_c{2 * tc}"), rs(f"mixT_c{2 * tc + 1}")] if kc < 4 else [rs("mixTc")]
                        op("pe", (lambda tc, kc, slot: lambda e: e.matmul(
                            TK[tc][:, :], lhsT=mixT[:, kc, tc * 128:(tc + 1) * 128], rhs=slot[:, kc * 512:(kc + 1) * 512],
                            start=(kc == 0), stop=(kc == 7)))(tc, kc, slot),
                           reads=[rslot] + rd_mix, writes=[r_TK[tc]])

                if half == 0:
                    for tc in range(ntc):
                        wo_mm(tc, range(0, 4))
                for tc in range(ntc):
                    wo_mm(tc, range(4, 8) if half == 0 else range(8))
                    xs_ = xb[xi][:, tc, half * 512:(half + 1) * 512]
                    op("dve", (lambda tc, xs_: lambda e: e.tensor_tensor(out=xs_, in0=TK[tc][:, :], in1=xs_, op=ALU.add))(tc, xs_),
                       reads=[r_TK[tc], r_x[xi][tc]], writes=[r_x[xi][tc]])
                    if half == 1:
                        norm_cast(xi, tc, True)
                        if tc >= 1:
                            norm_tr(tc - 1, 1)
                ws_release()
            norm_tr(ntc - 1, 1)

            stage(4)
            if last:
                if prompt:
                    for (src, rsrc, dst) in ((kf32, rs("kf32"), nkp), (vf32, rs("vf32"), nvp)):
                        t = next_tr()
                        og = next_tr()
                        op("pe", (lambda t, src: lambda e: e.transpose(out=TR[t][:, 0:128], in_=src[:, 0:128], identity=ident_f[:]))(t, src),
                           reads=[rsrc, rs("ident_f")], writes=[r_TR[t]])
                        op("act", (lambda t, og: lambda e: e.activation(out=ostg[og][:, 0:128], in_=TR[t][:, 0:128], func=AF.Copy))(t, og),
                           reads=[r_TR[t]], writes=[r_ostg[og]])
                        dma(dst[l, b, :, :], ostg[og][:, 0:128], [r_ostg[og], OUT], [])
                    t = next_tr()
                    og = next_tr()
                    for cc in range(4):
                        op("pe", (lambda t, cc: lambda e: e.transpose(out=TR[t][0:30, cc * 128:(cc + 1) * 128], in_=uf32[:, cc, 0:30],
                                                                      identity=ident_f[:]))(t, cc),
                           reads=[rs("uf32"), rs("ident_f")], writes=[r_TR[t]])
                    op("act", (lambda t, og: lambda e: e.activation(out=ostg[og][0:30, :], in_=TR[t][0:30, :], func=AF.Copy))(t, og),
                       reads=[r_TR[t]], writes=[r_ostg[og]])
                    dma(ncp[l, b, :, :], ostg[og][0:30, :], [r_ostg[og], OUT], [])
                else:
                    for (src, rsrc, dst, cache) in ((kf32, rs("kf32"), nks, ck), (vf32, rs("vf32"), nvs, cv)):
                        t = next_tr()
                        og = next_tr()
                        for bb in range(BS):
                            op("pe", (lambda t, bb, src: lambda e: e.transpose(out=TR[t][0:64, bb * 128:(bb + 1) * 128],
                                                                               in_=src[:, bb * 64:(bb + 1) * 64], identity=ident_f[:]))(t, bb, src),
                               reads=[rsrc, rs("ident_f")], writes=[r_TR[t]])
                        op("act", (lambda t, og: lambda e: e.activation(out=ostg[og][0:64, :], in_=TR[t][0:64, :], func=AF.Copy))(t, og),
                           reads=[r_TR[t]], writes=[r_ostg[og]])
                        dma(dst[l, :, 64:128, :].rearrange("b t f -> t b f"), ostg[og][0:64, :].rearrange("p (b f) -> p b f", b=BS),
                            [r_ostg[og], OUT], [])
                        dma(dst[l, :, 0:64, :], cache[l, :, 64:128, :], [OUT], [])
                    for bb in range(BS):
                        t = next_tr()
                        og = next_tr()
                        for cc in range(4):
                            op("pe", (lambda t, cc, bb: lambda e: e.transpose(out=TR[t][0:30, cc * 128:(cc + 1) * 128],
                                                                              in_=uf32[:, cc, bb * 30:(bb + 1) * 30],
                                                                              identity=ident_f[:]))(t, cc, bb),
                               reads=[rs("uf32"), rs("ident_f")], writes=[r_TR[t]])
                        op("act", (lambda t, og: lambda e: e.activation(out=ostg[og][0:30, :], in_=TR[t][0:30, :], func=AF.Copy))(t, og),
                           reads=[r_TR[t]], writes=[r_ostg[og]])
                        dma(ncs[l, bb, :, :], ostg[og][0:30, :], [r_ostg[og], OUT], [])

            for blk8 in range(8):
                slot, rslot = ws_get()
                for mi in range(4):
                    m = blk8 * 4 + mi
                    if blk8 == 0 and mi == 0 and NT == 512:
                        pre = {0: next_acc(), 1: next_acc()}
                        for h in range(2):
                            for mi2 in (0, 1):
                                for kc in range(8):
                                    op("pe", (lambda a2, kc, mi2, slot, h: lambda e: e.matmul(
                                        ACC[a2][:, h * 256:(h + 1) * 256],
                                        lhsT=slot[:, kc * 512 + mi2 * 128: kc * 512 + (mi2 + 1) * 128],
                                        rhs=xnT[1][:, kc, h * 256:(h + 1) * 256], start=(kc == 0), stop=(kc == 7)))(pre[mi2], kc, mi2, slot, h),
                                       reads=[rslot, r_xnTp[1][h]], writes=[r_ACC[pre[mi2]]])
                    if blk8 == 0 and mi < 2 and NT == 512:
                        a = pre[mi]
                    else:
                        a = next_acc()
                        for kc in range(8):
                            op("pe", (lambda a, kc, mi, slot: lambda e: e.matmul(
                                ACC[a][:, 0:NT], lhsT=slot[:, kc * 512 + mi * 128: kc * 512 + (mi + 1) * 128],
                                rhs=xnT[1][:, kc, 0:NT], start=(kc == 0), stop=(kc == 7)))(a, kc, mi, slot),
                               reads=[rslot, r_xnT[1]], writes=[r_ACC[a]])
                    op("act", (lambda a, m: lambda e: e.activation(out=rl[m % 2][:, 0:NT], in_=ACC[a][:, 0:NT], func=AF.Relu))(a, m),
                       reads=[r_ACC[a]], writes=[rs(f"rl{m % 2}")])
                    op("dve", (lambda m: lambda e: e.tensor_tensor(out=hidT[:, m, 0:NT], in0=rl[m % 2][:, 0:NT], in1=rl[m % 2][:, 0:NT],
                                                                   op=ALU.mult))(m),
                       reads=[rs(f"rl{m % 2}")], writes=[r_hid[m // 8]])
                ws_release()

            stage(5)
            norm_defer_tail(xi, ntc)
            for half in range(2):
                for kg in range(4):
                    slot, rslot = ws_get()
                    for tc in range(ntc):
                        for kcl in range(8):
                            kc = kg * 8 + kcl
                            op("pe", (lambda tc, kc, kcl, slot: lambda e: e.matmul(
                                TK[tc][:, :], lhsT=hidT[:, kc, tc * 128:(tc + 1) * 128], rhs=slot[:, kcl * 512:(kcl + 1) * 512],
                                start=(kc == 0), stop=(kc == 31)))(tc, kc, kcl, slot),
                               reads=[rslot, r_hid[kg]], writes=[r_TK[tc]])
                    ws_release()
                for tc in range(ntc):
                    xs_ = xb[xi][:, tc, half * 512:(half + 1) * 512]
                    op("dve", (lambda tc, xs_: lambda e: e.scalar_tensor_tensor(out=xs_, in0=TK[tc][:, :], scalar=rstd2[:, tc:tc + 1],
                                                                                in1=xs_, op0=ALU.mult, op1=ALU.add))(tc, xs_),
                       reads=[r_TK[tc], r_x[xi][tc], rs("rstd2")], writes=[r_x[xi][tc]])

        def final_norm(idx):
            kind, b, ti = tiles[idx]
            xi = idx % 2
            ntc = 4 if kind == "p" else 2
            norm_stats(xi, ntc, 1)
            for tc in range(ntc):
                op("dve", (lambda tc: lambda e: e.scalar_tensor_tensor(out=xb[xi][:, tc, :], in0=xb[xi][:, tc, :],
                                                                       scalar=rstd[:, tc:tc + 1], in1=gt[:, :],
                                                                       op0=ALU.mult, op1=ALU.mult))(tc),
                   reads=[r_x[xi][tc], rs("rstd"), rs("gt")], writes=[r_x[xi][tc]])
            if kind == "p":
                dma(yp[b, ti * 512:(ti + 1) * 512, :].rearrange("(tc p) d -> p tc d", p=128), xb[xi][:, :, :], r_x[xi] + [OUT], [])
            else:
                dma(ys[:, :].rearrange("(tc p) d -> p tc d", p=128), xb[xi][:, 0:2, :], r_x[xi][0:2] + [OUT], [])

        if not DBG_NOTILES:
            x_load(0)
            stage_load(0, 0)
            stage_load(0, 1)
            for _ in range(4):
                ws_issue()
        try:
            if DBG_NOTILES:
                raise _StopBuild()
            for idx in range(len(tiles)):
                if idx >= 1 and idx + 1 < len(tiles):
                    x_load(idx + 1)
                for l in range(L):
                    layer(idx, l)
                if idx == 0 and len(tiles) > 1:
                    x_load(1)
                final_norm(idx)
        except _StopBuild:
            pass
        op("sp", None, writes=[OUT] + list(R.values()))

        with nc.Block() as block:
            T.emit(block, sems, dsems)
    return nc


def _prep_weights(norm1, w_in, conv_w, conv_b, conv_ln_g, conv_ln_b, w_out, norm2, w_up, w_down):
    wsrc = np.empty((L * NBLK, 128, 4096), np.float32)
    qcols = [np.r_[j * 64:(j + 1) * 64, (j + 4) * 64:(j + 5) * 64] for j in range(4)]
    mcols = qcols + [np.arange(512, 640), np.arange(640, 768)]
    for cc in range(4):
        mcols.append(np.arange(1280 + cc * 128, 1280 + (cc + 1) * 128))
        mcols.append(np.arange(768 + cc * 128, 768 + (cc + 1) * 128))
    orow = np.concatenate([np.r_[j * 64:(j + 1) * 64, (j + 4) * 64:(j + 5) * 64] for j in range(4)] + [np.arange(512, 1024)])
    for l in range(L):
        base = l * NBLK
        wi = w_in[l].reshape(8, 128, 1792)
        for blk in range(4):
            buf = np.zeros((128, 8, 4, 128), np.float32)
            for mi in range(4):
                m = blk * 4 + mi
                if m < 14:
                    buf[:, :, mi, :] = wi[:, :, mcols[m]].transpose(1, 0, 2)
            wsrc[base + blk] = buf.reshape(128, 4096)
        wo = w_out[l][orow].reshape(8, 128, 1024)
        for half in range(2):
            wsrc[base + 4 + half] = wo[:, :, half * 512:(half + 1) * 512].transpose(1, 0, 2).reshape(128, 4096)
        wu = w_up[l].reshape(8, 128, 32, 128)
        for blk in range(8):
            wsrc[base + 6 + blk] = wu[:, :, blk * 4:(blk + 1) * 4, :].transpose(1, 0, 2, 3).reshape(128, 4096)
        wd = w_down[l].reshape(4, 8, 128, 1024)
        for half in range(2):
            for kg in range(4):
                wsrc[base + 14 + half * 4 + kg] = wd[kg][:, :, half * 512:(half + 1) * 512].transpose(1, 0, 2).reshape(128, 4096)
    gtab = np.empty((128, L * 16), np.float32)
    cw = np.empty((128, L * 4 * 31), np.float32)
    cpar = np.empty((128, L * 12), np.float32)
    for l in range(L):
        gtab[:, l * 16:l * 16 + 8] = norm1[l].reshape(8, 128).T
        gtab[:, l * 16 + 8:l * 16 + 16] = norm2[l].reshape(8, 128).T
        cw[:, l * 124:(l + 1) * 124] = conv_w[l].T.reshape(4, 128, 31).transpose(1, 0, 2).reshape(128, 124)
        cpar[:, l * 12:l * 12 + 4] = conv_b[l].reshape(4, 128).T
        cpar[:, l * 12 + 4:l * 12 + 8] = conv_ln_g[l].reshape(4, 128).T
        cpar[:, l * 12 + 8:l * 12 + 12] = conv_ln_b[l].reshape(4, 128).T
    return wsrc, gtab, cw, cpar


_PROG_CACHE = {}


def kernel(x_prompt, x_sample, cache_k, cache_v, state_conv, norm1, w_in, attn_sink, conv_w,
           conv_b, conv_ln_g, conv_ln_b, w_out, norm2, w_up, w_down, final_norm):
    f = lambda a: np.ascontiguousarray(np.asarray(a, dtype=np.float32))
    x_prompt, x_sample, cache_k, cache_v, state_conv = map(f, (x_prompt, x_sample, cache_k, cache_v, state_conv))
    B, S, _ = x_prompt.shape
    DB, TS, _ = x_sample.shape
    BP, BS = B // NCORES, DB // NCORES
    assert TS == 64
    wsrc, gtab, cw, cpar = _prep_weights(*map(f, (norm1, w_in, conv_w, conv_b, conv_ln_g, conv_ln_b, w_out, norm2, w_up, w_down)))
    fng = f(final_norm).reshape(1, D)
    sink = f(attn_sink).reshape(1, L * 8)
    key = (S, BP, BS)
    if key not in _PROG_CACHE:
        _PROG_CACHE[key] = build_program(S, BP, BS)
    nc = _PROG_CACHE[key]
    win = cache_k.shape[2]
    in_maps = []
    for c in range(NCORES):
        in_maps.append({
            "xp": x_prompt[c * BP:(c + 1) * BP],
            "xs": x_sample[c * BS:(c + 1) * BS].reshape(BS * 64, D),
            "ck": np.ascontiguousarray(cache_k[:, c * BS:(c + 1) * BS].reshape(L, BS, win, 128)),
            "cv": np.ascontiguousarray(cache_v[:, c * BS:(c + 1) * BS].reshape(L, BS, win, 128)),
            "sc": np.ascontiguousarray(state_conv[:, c * BS:(c + 1) * BS]),
            "wsrc": wsrc, "gtab": gtab, "cw": cw, "cpar": cpar, "fng": fng, "sink": sink,
        })
    res = run_bass_kernel_spmd(nc, in_maps, core_ids=list(range(NCORES)))
    rr = res.results
    cat = lambda name, axis: np.concatenate([np.asarray(r[name]) for r in rr], axis=axis)
    y_prompt = cat("yp", 0)
    y_sample = cat("ys", 0).reshape(DB, TS, D)
    nkp = cat("nkp", 1).reshape(L, B, 128, 2, 64)
    nvp = cat("nvp", 1).reshape(L, B, 128, 2, 64)
    ncp = cat("ncp", 1)
    nks = cat("nks", 1).reshape(L, DB, 128, 2, 64)
    nvs = cat("nvs", 1).reshape(L, DB, 128, 2, 64)
    ncs = cat("ncs", 1)
    return (y_prompt, y_sample, nkp, nvp, ncp, nks, nvs, ncs)
```
